# Optimizing a Trainium2 kernel written in Bass

```python
import jax, jax.numpy as jnp
from jax import lax
import numpy as np

D_MODEL = 2048
BATCH = 4
SEQ = 4096
DEPTH = 4

CHUNK = 64
EPS = 1e-6
D_PLE = 256
D_FF = 5632
SGU_BLOCK = 128
D_A = 1024
N_GROUPS_A = 8
DG_A = D_A // N_GROUPS_A
D_B = 512
CONV_W = 3
D_C = 512
POOL_WINDOWS = (2, 4, 8, 16)
N_GROUPS_C = len(POOL_WINDOWS)
DG_C = D_C // N_GROUPS_C
N_BRANCH = 3
D_MIX = D_A + D_B + D_C
N_IN = 2 * D_A + 3 * D_B + D_C + N_BRANCH * D_MODEL

kernel_name = "hybrid_gmlp_shortconv_pool_macaron_trunk"


def _rmsnorm(x, g):
    xf = x.astype(jnp.float32)
    y = xf * lax.rsqrt(jnp.mean(xf * xf, axis=-1, keepdims=True) + EPS)
    return y.astype(x.dtype) * g


def _layernorm(x, g, b):
    xf = x.astype(jnp.float32)
    mu = jnp.mean(xf, axis=-1, keepdims=True)
    var = jnp.mean(jnp.square(xf - mu), axis=-1, keepdims=True)
    y = (xf - mu) * lax.rsqrt(var + EPS)
    return y.astype(x.dtype) * g + b


def _swiglu(x, w_gate, w_up, w_down):
    return (jax.nn.silu(x @ w_gate) * (x @ w_up)) @ w_down


def _sgu_mixer(z, sgu_norm_g, sgu_norm_b, sgu_w, sgu_b):
    bsz, seq, _ = z.shape
    u, v = z[..., :D_A], z[..., D_A:]
    v = _layernorm(v, sgu_norm_g, sgu_norm_b)
    v = v.reshape(bsz, seq // SGU_BLOCK, SGU_BLOCK, N_GROUPS_A, DG_A)
    pos = jnp.arange(SGU_BLOCK)
    mask = (pos[None, :] // CHUNK) <= (pos[:, None] // CHUNK)
    w = jnp.where(mask[None], sgu_w, jnp.zeros_like(sgu_w))
    vm = jnp.einsum('gts,bnsgc->bntgc', w, v) + jnp.transpose(sgu_b)[:, :, None]
    return u * vm.reshape(bsz, seq, D_A)


def _short_conv_mixer(zb, conv_w):
    b_gate, c_gate, xin = zb[..., :D_B], zb[..., D_B:2 * D_B], zb[..., 2 * D_B:]
    y = c_gate * xin
    y = lax.conv_general_dilated(
        y, conv_w[:, None, :].astype(y.dtype), window_strides=(1,), padding=[(CONV_W - 1, 0)],
        dimension_numbers=('NWC', 'WIO', 'NWC'), feature_group_count=D_B)
    return b_gate * y


def _pool_mixer(xc, pool_w, pool_scale):
    bsz, seq, _ = xc.shape
    xf = xc.astype(jnp.float32)
    cs = jnp.pad(jnp.cumsum(xf, axis=1), ((0, 0), (1, 0), (0, 0)))
    t1 = jnp.arange(1, seq + 1, dtype=jnp.float32)
    outs = []
    for g, w in enumerate(POOL_WINDOWS):
        csg = cs[:, :, g * DG_C:(g + 1) * DG_C]
        hi = csg[:, 1:]
        lo = jnp.pad(csg[:, :seq + 1 - w], ((0, 0), (w - 1, 0), (0, 0)))
        cnt = jnp.minimum(t1, jnp.float32(w))[None, :, None]
        outs.append((hi - lo) / cnt)
    pooled = (jnp.concatenate(outs, axis=-1) - xf).astype(xc.dtype)
    pooled = pooled.reshape(bsz, seq, N_GROUPS_C, DG_C)
    y = jnp.einsum('bsgc,gcd->bsgd', pooled, pool_w).reshape(bsz, seq, D_C)
    return y * pool_scale


def _layer(x, p_i, ffn1_norm, ffn1_w_gate, ffn1_w_up, ffn1_w_down, mix_norm, w_in, sgu_norm_g, sgu_norm_b,
           sgu_w, sgu_b, conv_w, pool_w, pool_scale, w_branch_a, w_branch_b, w_branch_c, w_out,
           ffn2_norm, ffn2_w_gate, ffn2_w_up, ffn2_w_down, ple_norm, ple_w_gate, ple_w_proj):
    bsz, seq, _ = x.shape
    h = x + 0.5 * _swiglu(_rmsnorm(x, ffn1_norm), ffn1_w_gate, ffn1_w_up, ffn1_w_down)
    n = _rmsnorm(h, mix_norm)
    z = n @ w_in
    o1 = 2 * D_A
    o2 = o1 + 3 * D_B
    o3 = o2 + D_C
    ya = _sgu_mixer(jax.nn.gelu(z[..., :o1], approximate=False), sgu_norm_g, sgu_norm_b, sgu_w, sgu_b)
    yb = _short_conv_mixer(z[..., o1:o2], conv_w)
    yc = _pool_mixer(z[..., o2:o3], pool_w, pool_scale)
    gates = jax.nn.sigmoid(z[..., o3:]).reshape(bsz, seq, N_BRANCH, D_MODEL)
    m = (gates[:, :, 0] * (ya @ w_branch_a)
         + gates[:, :, 1] * (yb @ w_branch_b)
         + gates[:, :, 2] * (yc @ w_branch_c))
    h = h + m @ w_out
    h = h + 0.5 * _swiglu(_rmsnorm(h, ffn2_norm), ffn2_w_gate, ffn2_w_up, ffn2_w_down)
    h = h + jax.nn.sigmoid(_rmsnorm(h, ple_norm) @ ple_w_gate) * (p_i @ ple_w_proj)
    return h


def setup_inputs(seed: int = 0) -> dict:
    key = jax.random.key(seed)
    ks = iter(jax.random.split(key, 40))
    L, D, F = DEPTH, D_MODEL, D_FF

    def nrm(shape, scale):
        return jax.random.normal(next(ks), shape, jnp.float32) * scale

    def gain(shape):
        return 1.0 + nrm(shape, 0.01)

    return {
        "x": nrm((BATCH, SEQ, D), 1.0),
        "p": nrm((L, BATCH, SEQ, D_PLE), 1.0),
        "ffn1_norm": gain((L, D)),
        "ffn1_w_gate": nrm((L, D, F), D ** -0.5),
        "ffn1_w_up": nrm((L, D, F), D ** -0.5),
        "ffn1_w_down": nrm((L, F, D), F ** -0.5),
        "mix_norm": gain((L, D)),
        "w_in": nrm((L, D, N_IN), D ** -0.5),
        "sgu_norm_g": gain((L, D_A)),
        "sgu_norm_b": nrm((L, D_A), 0.01),
        "sgu_w": nrm((L, N_GROUPS_A, SGU_BLOCK, SGU_BLOCK), SGU_BLOCK ** -0.5),
        "sgu_b": gain((L, N_GROUPS_A, SGU_BLOCK)),
        "conv_w": nrm((L, CONV_W, D_B), CONV_W ** -0.5),
        "pool_w": nrm((L, N_GROUPS_C, DG_C, DG_C), DG_C ** -0.5),
        "pool_scale": gain((L, D_C)),
        "w_branch_a": nrm((L, D_A, D), D_A ** -0.5),
        "w_branch_b": nrm((L, D_B, D), D_B ** -0.5),
        "w_branch_c": nrm((L, D_C, D), D_C ** -0.5),
        "w_out": nrm((L, D, D), D ** -0.5),
        "ffn2_norm": gain((L, D)),
        "ffn2_w_gate": nrm((L, D, F), D ** -0.5),
        "ffn2_w_up": nrm((L, D, F), D ** -0.5),
        "ffn2_w_down": nrm((L, F, D), F ** -0.5),
        "ple_norm": gain((L, D)),
        "ple_w_gate": nrm((L, D, D), D ** -0.5),
        "ple_w_proj": nrm((L, D_PLE, D), D_PLE ** -0.5),
        "final_norm": gain((D,)),
    }


def reference(x, p, ffn1_norm, ffn1_w_gate, ffn1_w_up, ffn1_w_down, mix_norm, w_in, sgu_norm_g, sgu_norm_b,
              sgu_w, sgu_b, conv_w, pool_w, pool_scale, w_branch_a, w_branch_b, w_branch_c, w_out,
              ffn2_norm, ffn2_w_gate, ffn2_w_up, ffn2_w_down, ple_norm, ple_w_gate, ple_w_proj, final_norm):
    h = x
    for i in range(DEPTH):
        h = _layer(h, p[i], ffn1_norm[i], ffn1_w_gate[i], ffn1_w_up[i], ffn1_w_down[i], mix_norm[i], w_in[i],
                   sgu_norm_g[i], sgu_norm_b[i], sgu_w[i], sgu_b[i], conv_w[i], pool_w[i], pool_scale[i],
                   w_branch_a[i], w_branch_b[i], w_branch_c[i], w_out[i], ffn2_norm[i], ffn2_w_gate[i],
                   ffn2_w_up[i], ffn2_w_down[i], ple_norm[i], ple_w_gate[i], ple_w_proj[i])
    return _rmsnorm(h, final_norm)
```

```python
import numpy as np
from contextlib import ExitStack
import concourse.bass as bass
import concourse.mybir as mybir
from concourse.bass_utils import run_bass_kernel_spmd

F32 = mybir.dt.float32
BF16 = mybir.dt.bfloat16
AF = mybir.ActivationFunctionType
ALU = mybir.AluOpType

D = 2048
FF = 5632
DA = 1024
DB = 512
DC = 512
NIN = 10240
DPLE = 256
EPS = 1e-6
WINDOWS = (2, 4, 8, 16)
EPOCH = 60000
SLABW = 256
SLABE = 4096


class Res:
    __slots__ = ("name", "w", "r")

    def __init__(self, name):
        self.name = name
        self.w = None
        self.r = {}


class _Eng:
    def __init__(self, key):
        self.key = key
        self.count = 0
        self.waited = {}
        self.thunks = []


class Sched:
    ENGS = ("pe", "act", "dve", "pool", "sp")

    def __init__(self, nc):
        self.nc = nc
        self.E = {k: _Eng(k) for k in self.ENGS}
        self.dma_sems = {}

    def _deps(self, reads, writes):
        d = {}
        for R in reads:
            if R.w is not None:
                k, v = R.w
                if v > d.get(k, 0):
                    d[k] = v
        for R in writes:
            if R.w is not None:
                k, v = R.w
                if v > d.get(k, 0):
                    d[k] = v
            for k, v in R.r.items():
                if v > d.get(k, 0):
                    d[k] = v
        return d

    def _waits(self, E, deps, nosync_same):
        ws = []
        for k, v in deps.items():
            if E.waited.get(k, 0) >= v:
                continue
            if k == E.key and nosync_same:
                continue
            E.waited[k] = v
            ws.append((k, v))
        return ws

    def op(self, ename, fn, reads=(), writes=()):
        E = self.E[ename]
        ws = self._waits(E, self._deps(reads, writes), ename == "pe")
        E.count += 1
        c = E.count
        E.thunks.append((ws, fn, (ename, c)))
        for R in reads:
            R.r[ename] = c
        for R in writes:
            R.w = (ename, c)
            R.r = {}
        return c

    def dma(self, qname, dkey, out, in_, reads=(), writes=()):
        E = self.E[qname]
        deps = self._deps(reads, writes)
        deps.pop(("dma", dkey), None)
        ws = self._waits(E, deps, False)
        n = self.dma_sems.get(dkey, 0) + 1
        self.dma_sems[dkey] = n
        key = ("dma", dkey)
        E.thunks.append((ws, (lambda e, out=out, in_=in_: e.dma_start(out=out, in_=in_)), (key, n)))
        for R in reads:
            R.r[key] = n
        for R in writes:
            R.w = (key, n)
            R.r = {}
        return n

    def emit(self):
        nc = self.nc
        with ExitStack() as st:
            sems = {}
            for k in self.ENGS:
                E = self.E[k]
                ne = max(1, (E.count + EPOCH - 1) // EPOCH)
                for ep in range(ne):
                    sems[(k, ep)] = st.enter_context(nc.semaphore(f"s_{k}_{ep}"))
            for dk in self.dma_sems:
                sems[("dma", dk)] = st.enter_context(nc.semaphore(f"d_{dk}"))

            def sem_val(k, v):
                if isinstance(k, tuple):
                    return sems[k], 16 * v
                ep = (v - 1) // EPOCH
                return sems[(k, ep)], v - ep * EPOCH

            block = st.enter_context(nc.Block())

            def run(E):
                def body(eng):
                    for ws, fn, sig in E.thunks:
                        for k, v in ws:
                            s, val = sem_val(k, v)
                            eng.wait_ge(s, val)
                        ins = fn(eng)
                        k, c = sig
                        if isinstance(k, tuple):
                            ins.then_inc(sems[k], 16)
                        else:
                            s, _ = sem_val(k, c)
                            ins.then_inc(s, 1)
                    if E.key == "sp":
                        for dk, n in self.dma_sems.items():
                            eng.wait_ge(sems[("dma", dk)], 16 * n)
                return body

            block.tensor(run(self.E["pe"]))
            block.scalar(run(self.E["act"]))
            block.vector(run(self.E["dve"]))
            block.gpsimd(run(self.E["pool"]))
            block.sync(run(self.E["sp"]))


def smallp_layout(L):
    off = {}
    o = 0
    off["gain"] = o
    o += L * 4 * 16
    off["final"] = o
    o += 16
    off["conv"] = o
    o += L * 12
    off["pscale"] = o
    o += L * 4
    off["rc"] = o
    o += 64
    off["n"] = o
    return off


WEIGHT_SPECS = lambda L: [
    ("ffn1_w_gate", [L, D, FF]), ("ffn1_w_up", [L, D, FF]), ("ffn1_w_down", [L, FF, D]),
    ("w_in", [L, D, NIN]),
    ("w_branch_a", [L, DA, D]), ("w_branch_b", [L, DB, D]), ("w_branch_c", [L, DC, D]),
    ("w_out", [L, D, D]),
    ("ffn2_w_gate", [L, D, FF]), ("ffn2_w_up", [L, D, FF]), ("ffn2_w_down", [L, FF, D]),
    ("ple_w_gate", [L, D, D]), ("ple_w_proj", [L, DPLE, D]),
    ("pool_w", [L, 4, 128, 128]),
]


def build_program(L, TTS, NS=6, PHASES=("ffn1", "mix", "ffn2", "ple")):
    T = sum(TTS)
    TM = max(TTS)
    nc = bass.Bass("TRN2", target_bir_lowering=False)
    xT = nc.dram_tensor("xT", [D, T], F32, kind="ExternalInput").ap()
    pT = nc.dram_tensor("pT", [L, DPLE, T], F32, kind="ExternalInput").ap()
    outT = nc.dram_tensor("outT", [D, T], F32, kind="ExternalOutput").ap()
    W = {n: nc.dram_tensor(n, s, F32, kind="ExternalInput").ap() for n, s in WEIGHT_SPECS(L)}
    SO = smallp_layout(L)
    smallp_d = nc.dram_tensor("smallp", [128, SO["n"]], F32, kind="ExternalInput").ap()
    sgurep_d = nc.dram_tensor("sgurep", [L, 128, 3072], F32, kind="ExternalInput").ap()
    sguwT_d = nc.dram_tensor("sguwT", [L, 128, 1024], F32, kind="ExternalInput").ap()

    S = Sched(nc)

    with ExitStack() as st:
        def sb(name, shape, dt):
            return st.enter_context(nc.sbuf_tensor(name, shape, dt))

        resid = sb("resid", [128, 16, TM], F32)
        nT = sb("nT", [128, 16, TM], BF16)
        yT = sb("yT", [128, 16, TM], BF16)
        scrA = sb("scrA", [128, 16 * TM // 2], F32)
        scrA_bf = scrA.bitcast(BF16)
        ring = [sb(f"ring{i}", [128, SLABE], BF16) for i in range(NS)]
        smallp = sb("smallp_s", [128, SO["n"]], F32)
        sgurep = sb("sgurep_s", [128, 3072], F32)
        sguw = sb("sguw", [128, 1024], BF16)
        poolw = sb("poolw", [128, 512], BF16)
        hist = sb("hist", [128, L * 8 * 16], F32)
        pTs = sb("pTs", [128, 2, TM], BF16)
        rstd = sb("rstd", [128, TM], F32)
        sqt = [sb(f"sqt{i}", [128, 384], BF16) for i in range(2)]
        atmp = [sb(f"atmp{i}", [128, 512], F32) for i in range(4)]
        onesD = sb("onesD", [128, 128], BF16)
        mask = sb("mask", [128, 128], F32)
        lnst = sb("lnst", [128, 16], F32)
        epsc = sb("epsc", [128, 8], F32)
        psum = st.enter_context(nc.psum_tensor("psum", [128, 8 * 512], F32))

        r_res = [Res(f"res{c}") for c in range(16)]
        r_n = [Res(f"n{c}") for c in range(16)]
        r_y = [Res(f"y{c}") for c in range(16)]
        r_s = [Res(f"s{c}") for c in range(16)]
        r_ring = [Res(f"ring{i}") for i in range(NS)]
        r_small = Res("smallp")
        r_sgurep = Res("sgurep")
        r_sguw = Res("sguw")
        r_poolw = Res("poolw")
        r_hist = Res("hist")
        r_pTs = Res("pTs")
        r_rstd = Res("rstd")
        r_rh = [r_rstd, Res("rstd_h1")]
        r_sqt = [Res("sqt0"), Res("sqt1")]
        r_atmp = [Res(f"atmp{i}") for i in range(4)]
        r_const = Res("const")
        r_lnst = Res("lnst")
        r_bank = [Res(f"bank{i}") for i in range(8)]

        CH = TM // 2

        def sreg(lo, n):
            c0 = lo // CH
            c1 = (lo + n - 1) // CH
            return scrA[:, lo:lo + n], [r_s[c] for c in range(c0, c1 + 1)]

        HW_ = 16 + TM
        vtok, rs_vtok = sreg(0, 1024)
        _, rs_vln0 = sreg(1024, 512)
        _, rs_vln1 = sreg(1536, 512)
        vlnb = [(scrA_bf[:, 2048:3072], rs_vln0), (scrA_bf[:, 3072:4096], rs_vln1)]
        cx, rs_cx = sreg(2048, HW_)
        pin, rs_pin = sreg(2048 + HW_, HW_)
        pa, rs_pa = sreg(2048 + 2 * HW_, HW_)
        pb, rs_pb = sreg(2048 + 3 * HW_, HW_)
        o_pl = 2048 + 4 * HW_
        assert o_pl + TM // 2 <= 16 * CH, (o_pl, TM)
        pooledT = scrA_bf[:, 2 * o_pl:2 * o_pl + TM]
        _, rs_pooled = sreg(o_pl, TM // 2)

        def mTc(j, lo, n):
            return scrA_bf[:, j * TM + lo:j * TM + lo + n]

        bank_ctr = [0]

        def bank(n=1):
            b = bank_ctr[0]
            if n == 2 and b % 2 == 1:
                b += 1
            b %= 8
            bank_ctr[0] = b + n
            return b, [r_bank[b + i] for i in range(n)]

        def PS(b, lo, n):
            return psum[:, b * 512 + lo:b * 512 + lo + n]

        def mm(out, lhsT, rhs, start, stop, reads, writes):
            S.op("pe", lambda e: e.matmul(out, lhsT=lhsT, rhs=rhs, start=start, stop=stop), reads, writes)

        def act(out, in_, func, reads, writes, scale=None):
            if scale is None:
                S.op("act", lambda e: e.activation(out=out, in_=in_, func=func), reads, writes)
            else:
                S.op("act", lambda e: e.activation(out=out, in_=in_, func=func, scale=scale), reads, writes)

        def tt(out, in0, in1, op, reads, writes, eng="dve"):
            S.op(eng, lambda e: e.tensor_tensor(out=out, in0=in0, in1=in1, op=op), reads, writes)

        def stt(out, in0, scalar, in1, op0, op1, reads, writes, eng="dve"):
            S.op(eng, lambda e: e.scalar_tensor_tensor(out=out, in0=in0, scalar=scalar, in1=in1, op0=op0, op1=op1),
                 reads, writes)

        def ts(out, in0, s1, s2, op0, op1, reads, writes, eng="dve"):
            if s2 is None:
                S.op(eng, lambda e: e.tensor_scalar(out=out, in0=in0, scalar1=s1, scalar2=None, op0=op0), reads, writes)
            else:
                S.op(eng, lambda e: e.tensor_scalar(out=out, in0=in0, scalar1=s1, scalar2=s2, op0=op0, op1=op1),
                     reads, writes)

        def rsqrt(out, in_, reads, writes):
            S.op("act", lambda e: e.activation(out=out, in_=in_, func=AF.Sqrt, bias=epsc[:, 0:1]), list(reads) + [r_const], writes)
            S.op("dve", lambda e: e.reciprocal(out=out, in_=out), writes, writes)

        def cp(out, in_, reads, writes, eng="dve"):
            S.op(eng, lambda e: e.tensor_copy(out=out, in_=in_), reads, writes)

        slabs = []

        def slab_cols(w, l, c0, width=SLABW, k=16):
            return [(0, k, width, w[l, :, c0:c0 + width])]

        def slab_rows(w, l, r0, k, width):
            return [(0, k, width, w[l, r0:r0 + k * 128, :])]

        def gen_slabs():
            for _st in range(len(TTS)):
                for l in range(L):
                    for pre in PHASES:
                        if pre in ("ffn1", "ffn2"):
                            wg, wu, wd = W[pre + "_w_gate"], W[pre + "_w_up"], W[pre + "_w_down"]
                            for fg in range(FF // 512):
                                c0 = fg * 512
                                slabs.append(slab_cols(wg, l, c0))
                                slabs.append(slab_cols(wu, l, c0))
                                slabs.append(slab_cols(wg, l, c0 + 256))
                                slabs.append(slab_cols(wu, l, c0 + 256))
                                slabs.append(slab_rows(wd, l, c0, 2, 2048))
                                slabs.append(slab_rows(wd, l, c0 + 256, 2, 2048))
                        elif pre == "mix":
                            wi = W["w_in"]
                            for sbi in range(8):
                                slabs.append(slab_cols(wi, l, sbi * 256))
                            for pr in range(2):
                                slabs.append(slab_cols(wi, l, 2560 + pr * 256))
                                slabs.append(slab_cols(wi, l, 3072 + pr * 256))
                                slabs.append(slab_cols(wi, l, 2048 + pr * 256))
                            for pr in range(2):
                                slabs.append(slab_cols(wi, l, 3584 + pr * 256))
                            for jp in range(8):
                                for k in range(3):
                                    slabs.append(slab_cols(wi, l, 4096 + k * 2048 + jp * 256))
                                slabs.append([
                                    (0, 8, 256, W["w_branch_a"][l, :, jp * 256:(jp + 1) * 256]),
                                    (8 * 256, 4, 256, W["w_branch_b"][l, :, jp * 256:(jp + 1) * 256]),
                                    (12 * 256, 4, 256, W["w_branch_c"][l, :, jp * 256:(jp + 1) * 256]),
                                ])
                            for sbi in range(8):
                                slabs.append(slab_cols(W["w_out"], l, sbi * 256))
                        else:
                            for sbi in range(8):
                                slabs.append([(0, 2, 256, W["ple_w_proj"][l, :, sbi * 256:(sbi + 1) * 256])])
                                slabs.append(slab_cols(W["ple_w_gate"], l, sbi * 256))

        gen_slabs()
        ring_state = {"issued": 0, "next": 0}

        def issue_slab():
            i = ring_state["issued"]
            if i >= len(slabs):
                return
            slot = i % NS
            for (o, k, width, src) in slabs[i]:
                dst = ring[slot][:, o:o + k * width].rearrange("p (k f) -> p k f", k=k)
                S.dma("pool", f"ring{slot}", dst, src.rearrange("(k p) f -> p k f", p=128), writes=[r_ring[slot]])
            ring_state["issued"] = i + 1

        def next_slab():
            i = ring_state["next"]
            ring_state["next"] = i + 1
            assert i < ring_state["issued"], "ring underflow (need more slots resident than NS)"
            slot = i % NS
            return ring[slot], r_ring[slot]

        def release(n=1):
            for _ in range(n):
                issue_slab()

        S.dma("sp", "small", smallp[:, :], smallp_d[:, :], writes=[r_small])
        S.op("dve", lambda e: e.memset(onesD[:, :], 1.0 / D), [], [r_const])
        S.op("dve", lambda e: e.memset(epsc[:, :], EPS), [], [r_const])
        S.op("dve", lambda e: e.memset(mask[:, :], 1.0), [], [r_const])
        S.op("dve", lambda e: e.memset(mask[64:128, 0:64], 0.0), [], [r_const])
        for _ in range(NS):
            issue_slab()

        def gain_ap(l, n, c):
            o = SO["gain"] + (l * 4 + n) * 16 + c
            return smallp[:, o:o + 1]

        def coltiles(TT):
            h = TT // 2
            return [(0, h), (h, h)]

        def rmsnorm(TT, gain_of_c, out_fp32_inplace=False):
            for (c0, n) in coltiles(TT):
                b, rb = bank()
                for c in range(16):
                    q = c % 2
                    act(sqt[q][:, :n], resid[:, c, c0:c0 + n], AF.Square, [r_res[c]], [r_sqt[q]])
                    mm(PS(b, 0, n), onesD[:, :], sqt[q][:, :n], c == 0, c == 15, [r_const, r_sqt[q]], rb)
                rsqrt(rstd[:, c0:c0 + n], PS(b, 0, n), rb, r_rh)
                for c in range(16):
                    if out_fp32_inplace:
                        stt(resid[:, c, c0:c0 + n], resid[:, c, c0:c0 + n], gain_of_c(c), rstd[:, c0:c0 + n],
                            ALU.mult, ALU.mult, [r_res[c], r_small] + r_rh, [r_res[c]])
                    else:
                        stt(nT[:, c, c0:c0 + n], resid[:, c, c0:c0 + n], gain_of_c(c), rstd[:, c0:c0 + n],
                            ALU.mult, ALU.mult, [r_res[c], r_small] + r_rh, [r_n[c]])

        def ffn(TT, l, n_idx):
            rmsnorm(TT, lambda c: gain_ap(l, n_idx, c))
            ai = 0
            for fg in range(FF // 512):
                hb = fg % 2
                g01, rg01 = next_slab()
                u01, ru01 = next_slab()
                g23, rg23 = next_slab()
                u23, ru23 = next_slab()
                d01, rd01 = next_slab()
                d23, rd23 = next_slab()
                for q in range(4):
                    gs, rgs = (g01, rg01) if q < 2 else (g23, rg23)
                    us, rus = (u01, ru01) if q < 2 else (u23, ru23)
                    co = (q % 2) * 128
                    hc = hb * 4 + q
                    for (c0, n) in coltiles(TT):
                        bg, rbg = bank()
                        for kc in range(16):
                            mm(PS(bg, 0, n), gs[:, kc * 256 + co:kc * 256 + co + 128], nT[:, kc, c0:c0 + n],
                               kc == 0, kc == 15, [rgs, r_n[kc]], rbg)
                        bu, rbu = bank()
                        for kc in range(16):
                            mm(PS(bu, 0, n), us[:, kc * 256 + co:kc * 256 + co + 128], nT[:, kc, c0:c0 + n],
                               kc == 0, kc == 15, [rus, r_n[kc]], rbu)
                        a = ai % 4
                        ai += 1
                        act(atmp[a][:, :n], PS(bg, 0, n), AF.Silu, rbg, [r_atmp[a]])
                        tt(mTc(hc, c0, n), atmp[a][:, :n], PS(bu, 0, n), ALU.mult, [r_atmp[a]] + rbu, [r_s[hc]])
                    if q == 1 or q == 3:
                        release(2)
                for j in range(16):
                    for (c0, n) in coltiles(TT):
                        bd, rbd = bank()
                        for q in range(4):
                            ds, rds = (d01, rd01) if q < 2 else (d23, rd23)
                            o = (q % 2) * 2048 + j * 128
                            mm(PS(bd, 0, n), ds[:, o:o + 128], mTc(hb * 4 + q, c0, n), q == 0, q == 3,
                               [rds, r_s[hb * 4 + q]], rbd)
                        stt(resid[:, j, c0:c0 + n], PS(bd, 0, n), 0.5, resid[:, j, c0:c0 + n], ALU.mult, ALU.add,
                            rbd + [r_res[j]], [r_res[j]])
                release(2)

        def hist_ap(l, i):
            o = (l * 8 + i) * 16
            return hist[:, o:o + 16]

        def mixer(TT, l, st_i):
            rmsnorm(TT, lambda c: gain_ap(l, 1, c))
            cts = coltiles(TT)
            nblk = TT // 128
            S.dma("sp", "sgurep", sgurep[:, :], sgurep_d[l, :, :], writes=[r_sgurep])
            S.dma("sp", "stage", vtok, sguwT_d[l, :, :], writes=rs_vtok)
            for g in range(8):
                tt(sguw[:, g * 128:(g + 1) * 128], vtok[:, g * 128:(g + 1) * 128], mask[:, :], ALU.mult,
                   rs_vtok + [r_const], [r_sguw])
            S.dma("sp", "stage2", pa[:, 0:512].rearrange("p (g d) -> p g d", g=4),
                  W["pool_w"][l].rearrange("g c d -> c g d"), writes=rs_pa)
            cp(poolw[:, :], pa[:, 0:512], rs_pa, [r_poolw])
            ai = 0
            for sbi in range(4):
                sl, rsl = next_slab()
                for cc in range(2):
                    c = sbi * 2 + cc
                    for (c0, n) in cts:
                        b, rb = bank()
                        for kc in range(16):
                            mm(PS(b, 0, n), sl[:, kc * 256 + cc * 128:kc * 256 + cc * 128 + 128], nT[:, kc, c0:c0 + n],
                               kc == 0, kc == 15, [rsl, r_n[kc]], rb)
                        act(yT[:, c, c0:c0 + n], PS(b, 0, n), AF.Gelu, rb, [r_y[c]])
                release(1)
            vs = [next_slab() for _ in range(4)]

            def m2_front(blk):
                t0 = blk * 128
                vln, rs_vln = vlnb[blk % 2]
                b, rb = bank(2)
                for sv in range(4):
                    sl, rsl = vs[sv]
                    for kc in range(16):
                        mm(PS(b, sv * 256, 256), nT[:, kc, t0:t0 + 128], sl[:, kc * 256:(kc + 1) * 256],
                           kc == 0, kc == 15, [rsl, r_n[kc]], rb)
                act(vtok, PS(b, 0, 1024), AF.Gelu, rb, rs_vtok)
                S.op("dve", lambda e: e.bn_stats(out=lnst[:, 0:6], in_=vtok[:, 0:512]), rs_vtok, [r_lnst])
                S.op("dve", lambda e: e.bn_stats(out=lnst[:, 6:12], in_=vtok[:, 512:1024]), rs_vtok, [r_lnst])
                S.op("dve", lambda e: e.bn_aggr(out=lnst[:, 12:14], in_=lnst[:, 0:12]), [r_lnst], [r_lnst])
                rsqrt(lnst[:, 14:15], lnst[:, 13:14], [r_lnst], [r_lnst])
                stt(lnst[:, 15:16], lnst[:, 12:13], -1.0, lnst[:, 14:15], ALU.mult, ALU.mult, [r_lnst], [r_lnst])
                S.op("act", lambda e: e.activation(out=vtok, in_=vtok, func=AF.Identity, bias=lnst[:, 15:16],
                                                   scale=lnst[:, 14:15]), rs_vtok + [r_lnst], rs_vtok)
                tt(vtok, vtok, sgurep[:, 0:1024], ALU.mult, rs_vtok + [r_sgurep], rs_vtok)
                tt(vln, vtok, sgurep[:, 1024:2048], ALU.add, rs_vtok + [r_sgurep], rs_vln)

            def m2_back(blk):
                nonlocal ai
                t0 = blk * 128
                vln, rs_vln = vlnb[blk % 2]
                b2, rb2 = bank(2)
                for g in range(8):
                    mm(PS(b2, g * 128, 128), vln[:, g * 128:(g + 1) * 128], sguw[:, g * 128:(g + 1) * 128], True, True,
                       rs_vln + [r_sguw], rb2)
                for h in range(2):
                    a = ai % 4
                    ai += 1
                    tt(atmp[a][:, :], PS(b2, h * 512, 512), sgurep[:, 2048 + h * 512:2048 + (h + 1) * 512], ALU.add,
                       rb2 + [r_sgurep], [r_atmp[a]])
                    yv = yT[:, 4 * h:4 * h + 4, t0:t0 + 128]
                    tt(yv, atmp[a][:, :].rearrange("p (g t) -> p g t", g=4), yv, ALU.mult,
                       [r_atmp[a]] + r_y[4 * h:4 * h + 4], r_y[4 * h:4 * h + 4])

            for blk in range(nblk + 1):
                if blk < nblk:
                    m2_front(blk)
                if blk >= 1:
                    m2_back(blk - 1)
            release(4)
            for pr in range(2):
                sC, rC = next_slab()
                sX, rX = next_slab()
                sB, rB = next_slab()
                cvb = [(pa, rs_pa), (pb, rs_pb)]
                for cc in range(2):
                    c = pr * 2 + cc
                    wo = cc * 128
                    cv, rs_cv = cvb[cc]
                    if st_i == 0:
                        S.op("dve", lambda e: e.memset(cx[:, 0:16], 0.0), [], rs_cx)
                    else:
                        cp(cx[:, 0:16], hist_ap(l, c), [r_hist], rs_cx)
                    for (c0, n) in cts:
                        bC, rbC = bank()
                        for kc in range(16):
                            mm(PS(bC, 0, n), sC[:, kc * 256 + wo:kc * 256 + wo + 128], nT[:, kc, c0:c0 + n],
                               kc == 0, kc == 15, [rC, r_n[kc]], rbC)
                        bX, rbX = bank()
                        for kc in range(16):
                            mm(PS(bX, 0, n), sX[:, kc * 256 + wo:kc * 256 + wo + 128], nT[:, kc, c0:c0 + n],
                               kc == 0, kc == 15, [rX, r_n[kc]], rbX)
                        a = ai % 4
                        ai += 1
                        act(atmp[a][:, :n], PS(bC, 0, n), AF.Copy, rbC, [r_atmp[a]])
                        tt(cx[:, 16 + c0:16 + c0 + n], atmp[a][:, :n], PS(bX, 0, n), ALU.mult, [r_atmp[a]] + rbX, rs_cx)
                    cp(hist_ap(l, c), cx[:, TT:TT + 16], rs_cx, [r_hist])
                    cwo = SO["conv"] + l * 12 + c * 3
                    ts(cv[:, 0:TT], cx[:, 16:16 + TT], smallp[:, cwo + 2:cwo + 3], None, ALU.mult, None,
                       rs_cx + [r_small], rs_cv)
                    stt(cv[:, 0:TT], cx[:, 15:15 + TT], smallp[:, cwo + 1:cwo + 2], cv[:, 0:TT], ALU.mult, ALU.add,
                        rs_cx + rs_cv + [r_small], rs_cv)
                    stt(cv[:, 0:TT], cx[:, 14:14 + TT], smallp[:, cwo:cwo + 1], cv[:, 0:TT], ALU.mult, ALU.add,
                        rs_cx + rs_cv + [r_small], rs_cv)
                for cc in range(2):
                    c = pr * 2 + cc
                    wo = cc * 128
                    cv, rs_cv = cvb[cc]
                    for (c0, n) in cts:
                        bB, rbB = bank()
                        for kc in range(16):
                            mm(PS(bB, 0, n), sB[:, kc * 256 + wo:kc * 256 + wo + 128], nT[:, kc, c0:c0 + n],
                               kc == 0, kc == 15, [rB, r_n[kc]], rbB)
                        tt(yT[:, 8 + c, c0:c0 + n], cv[:, c0:c0 + n], PS(bB, 0, n), ALU.mult, rs_cv + rbB, [r_y[8 + c]])
                release(3)
            def m4_back(g):
                pso = SO["pscale"] + l * 4 + g
                for (c0, n) in cts:
                    b, rb = bank()
                    mm(PS(b, 0, n), poolw[:, g * 128:(g + 1) * 128], pooledT[:, c0:c0 + n], True, True,
                       [r_poolw] + rs_pooled, rb)
                    act(yT[:, 12 + g, c0:c0 + n], PS(b, 0, n), AF.Copy, rb + [r_small], [r_y[12 + g]],
                        scale=smallp[:, pso:pso + 1])

            pending = None
            for pr in range(2):
                sP, rP = next_slab()
                for cc in range(2):
                    g = pr * 2 + cc
                    wo = cc * 128
                    wdw = WINDOWS[g]
                    if st_i == 0:
                        S.op("dve", lambda e: e.memset(pin[:, 0:16], 0.0), [], rs_pin)
                    else:
                        cp(pin[:, 0:16], hist_ap(l, 4 + g), [r_hist], rs_pin)
                    for (c0, n) in cts:
                        b, rb = bank()
                        for kc in range(16):
                            mm(PS(b, 0, n), sP[:, kc * 256 + wo:kc * 256 + wo + 128], nT[:, kc, c0:c0 + n],
                               kc == 0, kc == 15, [rP, r_n[kc]], rb)
                        act(pin[:, 16 + c0:16 + c0 + n], PS(b, 0, n), AF.Copy, rb, rs_pin)
                    cp(hist_ap(l, 4 + g), pin[:, TT:TT + 16], rs_pin, [r_hist])
                    if pending is not None:
                        m4_back(pending)
                    E_ = 16 + TT
                    tt(pa[:, 1:E_], pin[:, 1:E_], pin[:, 0:E_ - 1], ALU.add, rs_pin, rs_pa)
                    cur, rcur = pa, rs_pa
                    if wdw >= 4:
                        tt(pb[:, 3:E_], pa[:, 3:E_], pa[:, 1:E_ - 2], ALU.add, rs_pa, rs_pb)
                        cur, rcur = pb, rs_pb
                    if wdw >= 8:
                        tt(pa[:, 7:E_], pb[:, 7:E_], pb[:, 3:E_ - 4], ALU.add, rs_pb, rs_pa)
                        cur, rcur = pa, rs_pa
                    if wdw >= 16:
                        tt(pb[:, 15:E_], pa[:, 15:E_], pa[:, 7:E_ - 8], ALU.add, rs_pa, rs_pb)
                        cur, rcur = pb, rs_pb
                    stt(pooledT[:, 0:TT], cur[:, 16:E_], 1.0 / wdw, pin[:, 16:E_], ALU.mult, ALU.subtract,
                        rcur + rs_pin, rs_pooled)
                    if st_i == 0:
                        a = ai % 4
                        ai += 1
                        ro = SO["rc"] + g * 16
                        tt(atmp[a][:, 0:16], cur[:, 16:32], smallp[:, ro:ro + 16], ALU.mult, rcur + [r_small], [r_atmp[a]])
                        tt(pooledT[:, 0:16], atmp[a][:, 0:16], pin[:, 16:32], ALU.subtract, [r_atmp[a]] + rs_pin, rs_pooled)
                    pending = g
                release(1)
            m4_back(pending)
            KCS = (range(0, 8), range(8, 12), range(12, 16))
            hw = TM // 2
            tmpb = [(rstd[:, 0:hw], r_rh[0]), (rstd[:, hw:2 * hw], r_rh[1])]
            ti = 0
            for jp in range(8):
                G = [next_slab() for _ in range(3)]
                BR, rBR = next_slab()
                for k in range(3):
                    sl, rsl = G[k]
                    kcs = KCS[k]
                    for jj in range(2):
                        j = jp * 2 + jj
                        wo = jj * 128
                        for ti_, (c0, n) in enumerate(cts):
                            acc = jj * 2 + ti_
                            bg, rbg = bank()
                            for kc in range(16):
                                mm(PS(bg, 0, n), sl[:, kc * 256 + wo:kc * 256 + wo + 128], nT[:, kc, c0:c0 + n],
                                   kc == 0, kc == 15, [rsl, r_n[kc]], rbg)
                            bb, rbb = bank()
                            for kc in kcs:
                                mm(PS(bb, 0, n), BR[:, kc * 256 + wo:kc * 256 + wo + 128], yT[:, kc, c0:c0 + n],
                                   kc == kcs[0], kc == kcs[-1], [rBR, r_y[kc]], rbb)
                            tb, rtb = tmpb[ti % 2]
                            ti += 1
                            act(tb[:, :n], PS(bg, 0, n), AF.Sigmoid, rbg, [rtb])
                            if k == 0:
                                tt(atmp[acc][:, :n], tb[:, :n], PS(bb, 0, n), ALU.mult, [rtb] + rbb, [r_atmp[acc]])
                            else:
                                tt(tb[:, :n], tb[:, :n], PS(bb, 0, n), ALU.mult, [rtb] + rbb, [rtb])
                                if k == 1:
                                    tt(atmp[acc][:, :n], atmp[acc][:, :n], tb[:, :n], ALU.add,
                                       [rtb, r_atmp[acc]], [r_atmp[acc]])
                                else:
                                    tt(mTc(j, c0, n), atmp[acc][:, :n], tb[:, :n], ALU.add,
                                       [rtb, r_atmp[acc]], [r_s[j]])
                    release(1 if k < 2 else 2)
            for sbi in range(8):
                sl, rsl = next_slab()
                for jj in range(2):
                    j = sbi * 2 + jj
                    wo = jj * 128
                    for (c0, n) in cts:
                        b, rb = bank()
                        for kc in range(16):
                            mm(PS(b, 0, n), sl[:, kc * 256 + wo:kc * 256 + wo + 128], mTc(kc, c0, n),
                               kc == 0, kc == 15, [rsl, r_s[kc]], rb)
                        tt(resid[:, j, c0:c0 + n], PS(b, 0, n), resid[:, j, c0:c0 + n], ALU.add, rb + [r_res[j]], [r_res[j]])
                release(1)

        def ple(TT, l, tok0):
            S.dma("pool", "pTs", pTs[:, :, 0:TT], pT[l, :, tok0:tok0 + TT].rearrange("(k p) t -> p k t", p=128),
                  writes=[r_pTs])
            rmsnorm(TT, lambda c: gain_ap(l, 3, c))
            ai = 0
            for sbi in range(8):
                pj, rpj = next_slab()
                sl, rsl = next_slab()
                for jj in range(2):
                    j = sbi * 2 + jj
                    wo = jj * 128
                    for (c0, n) in coltiles(TT):
                        bg, rbg = bank()
                        for kc in range(16):
                            mm(PS(bg, 0, n), sl[:, kc * 256 + wo:kc * 256 + wo + 128], nT[:, kc, c0:c0 + n],
                               kc == 0, kc == 15, [rsl, r_n[kc]], rbg)
                        bp, rbp = bank()
                        for kc in range(2):
                            mm(PS(bp, 0, n), pj[:, kc * 256 + wo:kc * 256 + wo + 128], pTs[:, kc, c0:c0 + n],
                               kc == 0, kc == 1, [rpj, r_pTs], rbp)
                        a = ai % 4
                        ai += 1
                        act(atmp[a][:, :n], PS(bg, 0, n), AF.Sigmoid, rbg, [r_atmp[a]])
                        tt(atmp[a][:, :n], atmp[a][:, :n], PS(bp, 0, n), ALU.mult, [r_atmp[a]] + rbp, [r_atmp[a]])
                        tt(resid[:, j, c0:c0 + n], atmp[a][:, :n], resid[:, j, c0:c0 + n], ALU.add,
                           [r_atmp[a], r_res[j]], [r_res[j]])
                release(2)

        tok0 = 0
        for st_i, TT in enumerate(TTS):
            S.dma("sp", "xin", resid[:, :, 0:TT], xT[:, tok0:tok0 + TT].rearrange("(c p) t -> p c t", p=128),
                  writes=r_res)
            for l in range(L):
                if "ffn1" in PHASES:
                    ffn(TT, l, 0)
                if "mix" in PHASES:
                    mixer(TT, l, st_i)
                if "ffn2" in PHASES:
                    ffn(TT, l, 2)
                if "ple" in PHASES:
                    ple(TT, l, tok0)
            fo = SO["final"]
            rmsnorm(TT, lambda c: smallp[:, fo + c:fo + c + 1], out_fp32_inplace=True)
            S.dma("sp", "xout", outT[:, tok0:tok0 + TT].rearrange("(c p) t -> p c t", p=128), resid[:, :, 0:TT],
                  reads=r_res)
            tok0 += TT
        assert ring_state["next"] == len(slabs), (ring_state, len(slabs))
        S.emit()
    return nc


def make_core_inputs(inputs, L, T, SEQ, n_batch):
    x = np.asarray(inputs["x"], dtype=np.float32)
    p = np.asarray(inputs["p"], dtype=np.float32)
    SO = smallp_layout(L)
    sm = np.zeros((128, SO["n"]), np.float32)
    names = ["ffn1_norm", "mix_norm", "ffn2_norm", "ple_norm"]
    for l in range(L):
        for n, nm in enumerate(names):
            g = np.asarray(inputs[nm], np.float32)[l].reshape(16, 128).T
            o = SO["gain"] + (l * 4 + n) * 16
            sm[:, o:o + 16] = g
        cw = np.asarray(inputs["conv_w"], np.float32)[l]
        o = SO["conv"] + l * 12
        sm[:, o:o + 12] = cw.reshape(3, 4, 128).transpose(2, 1, 0).reshape(128, 12)
        ps = np.asarray(inputs["pool_scale"], np.float32)[l]
        o = SO["pscale"] + l * 4
        sm[:, o:o + 4] = ps.reshape(4, 128).T
    sm[:, SO["final"]:SO["final"] + 16] = np.asarray(inputs["final_norm"], np.float32).reshape(16, 128).T
    rc_first = np.zeros((4, 16), np.float32)
    rc_other = np.zeros((4, 16), np.float32)
    for g, w in enumerate(WINDOWS):
        for i in range(16):
            rc_first[g, i] = 1.0 / min(i + 1, w)
            rc_other[g, i] = 1.0 / w
    sm_first = sm.copy()
    sm_first[:, SO["rc"]:SO["rc"] + 64] = rc_first.reshape(1, 64)
    sm_other = sm.copy()
    sm_other[:, SO["rc"]:SO["rc"] + 64] = rc_other.reshape(1, 64)

    g_ = np.asarray(inputs["sgu_norm_g"], np.float32)[:L]
    b_ = np.asarray(inputs["sgu_norm_b"], np.float32)[:L]
    sb_ = np.asarray(inputs["sgu_b"], np.float32)[:L].reshape(L, 1024)
    rep = np.concatenate([g_, b_, sb_], axis=1)
    sgurep = np.ascontiguousarray(np.broadcast_to(rep[:, None, :], (L, 128, 3072)))
    sw = np.asarray(inputs["sgu_w"], np.float32)[:L]
    sguwT = np.ascontiguousarray(sw.transpose(0, 3, 1, 2).reshape(L, 128, 1024))

    shared = {n: np.asarray(inputs[n], np.float32)[:L] for n, _ in WEIGHT_SPECS(L)}
    shared["sgurep"] = sgurep
    shared["sguwT"] = sguwT
    in_maps = []
    for b in range(n_batch):
        for h in range(2):
            t0 = 0 if h == 0 else SEQ - T
            m = dict(shared)
            m["xT"] = np.ascontiguousarray(x[b, t0:t0 + T, :].T)
            m["pT"] = np.ascontiguousarray(p[:L, b, t0:t0 + T, :].transpose(0, 2, 1))
            m["smallp"] = sm_first if h == 0 else sm_other
            in_maps.append(m)
    return in_maps


def run_config(inputs, L, TTS, SEQ, n_batch, trace=False, **bk):
    T = sum(TTS)
    nc = build_program(L, TTS, **bk)
    in_maps = make_core_inputs(inputs, L, T, SEQ, n_batch)
    ncores = len(in_maps)
    res = run_bass_kernel_spmd(nc, in_maps, core_ids=list(range(ncores)), **({"trace": True} if trace else {}))
    out = np.empty((n_batch, SEQ, D), np.float32)
    for b in range(n_batch):
        o0 = res.results[2 * b]["outT"]
        o1 = res.results[2 * b + 1]["outT"]
        out[b, 0:T, :] = o0.T
        out[b, T:SEQ, :] = o1[:, 2 * T - SEQ:].T
    return out, res


def kernel(**inputs):
    out, _ = run_config(inputs, 4, [768, 768, 640], 4096, 4)
    return out
```

```python
import numpy as np
from contextlib import ExitStack
import concourse.bass as bass
import concourse.mybir as mybir
from concourse.bass_utils import run_bass_kernel_spmd

F32 = mybir.dt.float32
BF16 = mybir.dt.bfloat16
AF = mybir.ActivationFunctionType
ALU = mybir.AluOpType

D = 2048
FF = 5632
DA = 1024
DB = 512
DC = 512
NIN = 10240
DPLE = 256
EPS = 1e-6
WINDOWS = (2, 4, 8, 16)
EPOCH = 60000
SLABW = 256
SLABE = 4096


class Res:
    __slots__ = ("name", "w", "r")

    def __init__(self, name):
        self.name = name
        self.w = None
        self.r = {}


class _Eng:
    def __init__(self, key):
        self.key = key
        self.count = 0
        self.waited = {}
        self.thunks = []


class Sched:
    ENGS = ("pe", "act", "dve", "pool", "sp")

    def __init__(self, nc):
        self.nc = nc
        self.E = {k: _Eng(k) for k in self.ENGS}
        self.dma_sems = {}

    def _deps(self, reads, writes):
        d = {}
        for R in reads:
            if R.w is not None:
                k, v = R.w
                if v > d.get(k, 0):
                    d[k] = v
        for R in writes:
            if R.w is not None:
                k, v = R.w
                if v > d.get(k, 0):
                    d[k] = v
            for k, v in R.r.items():
                if v > d.get(k, 0):
                    d[k] = v
        return d

    def _waits(self, E, deps, nosync_same):
        ws = []
        for k, v in deps.items():
            if E.waited.get(k, 0) >= v:
                continue
            if k == E.key and nosync_same:
                continue
            E.waited[k] = v
            ws.append((k, v))
        return ws

    def op(self, ename, fn, reads=(), writes=()):
        E = self.E[ename]
        ws = self._waits(E, self._deps(reads, writes), ename == "pe")
        E.count += 1
        c = E.count
        E.thunks.append((ws, fn, (ename, c)))
        for R in reads:
            R.r[ename] = c
        for R in writes:
            R.w = (ename, c)
            R.r = {}
        return c

    def dma(self, qname, dkey, out, in_, reads=(), writes=()):
        E = self.E[qname]
        deps = self._deps(reads, writes)
        deps.pop(("dma", dkey), None)
        ws = self._waits(E, deps, False)
        n = self.dma_sems.get(dkey, 0) + 1
        self.dma_sems[dkey] = n
        key = ("dma", dkey)
        E.thunks.append((ws, (lambda e, out=out, in_=in_: e.dma_start(out=out, in_=in_)), (key, n)))
        for R in reads:
            R.r[key] = n
        for R in writes:
            R.w = (key, n)
            R.r = {}
        return n

    def emit(self):
        nc = self.nc
        with ExitStack() as st:
            sems = {}
            for k in self.ENGS:
                E = self.E[k]
                ne = max(1, (E.count + EPOCH - 1) // EPOCH)
                for ep in range(ne):
                    sems[(k, ep)] = st.enter_context(nc.semaphore(f"s_{k}_{ep}"))
            for dk in self.dma_sems:
                sems[("dma", dk)] = st.enter_context(nc.semaphore(f"d_{dk}"))

            def sem_val(k, v):
                if isinstance(k, tuple):
                    return sems[k], 16 * v
                ep = (v - 1) // EPOCH
                return sems[(k, ep)], v - ep * EPOCH

            block = st.enter_context(nc.Block())

            def run(E):
                def body(eng):
                    for ws, fn, sig in E.thunks:
                        for k, v in ws:
                            s, val = sem_val(k, v)
                            eng.wait_ge(s, val)
                        ins = fn(eng)
                        k, c = sig
                        if isinstance(k, tuple):
                            ins.then_inc(sems[k], 16)
                        else:
                            s, _ = sem_val(k, c)
                            ins.then_inc(s, 1)
                    if E.key == "sp":
                        for dk, n in self.dma_sems.items():
                            eng.wait_ge(sems[("dma", dk)], 16 * n)
                return body

            block.tensor(run(self.E["pe"]))
            block.scalar(run(self.E["act"]))
            block.vector(run(self.E["dve"]))
            block.gpsimd(run(self.E["pool"]))
            block.sync(run(self.E["sp"]))


def smallp_layout(L):
    off = {}
    o = 0
    off["gain"] = o
    o += L * 4 * 16
    off["final"] = o
    o += 16
    off["conv"] = o
    o += L * 12
    off["pscale"] = o
    o += L * 4
    off["rc"] = o
    o += 64
    off["n"] = o
    return off


WEIGHT_SPECS = lambda L: [
    ("ffn1_w_gate", [L, D, FF]), ("ffn1_w_up", [L, D, FF]), ("ffn1_w_down", [L, FF, D]),
    ("w_in", [L, D, NIN]),
    ("w_branch_a", [L, DA, D]), ("w_branch_b", [L, DB, D]), ("w_branch_c", [L, DC, D]),
    ("w_out", [L, D, D]),
    ("ffn2_w_gate", [L, D, FF]), ("ffn2_w_up", [L, D, FF]), ("ffn2_w_down", [L, FF, D]),
    ("ple_w_gate", [L, D, D]), ("ple_w_proj", [L, DPLE, D]),
    ("pool_w", [L, 4, 128, 128]),
]


def build_program(L, TTS, NS=6, PHASES=("ffn1", "mix", "ffn2", "ple")):
    T = sum(TTS)
    TM = max(TTS)
    nc = bass.Bass("TRN2", target_bir_lowering=False)
    xT = nc.dram_tensor("xT", [D, T], F32, kind="ExternalInput").ap()
    pT = nc.dram_tensor("pT", [L, DPLE, T], F32, kind="ExternalInput").ap()
    outT = nc.dram_tensor("outT", [D, T], F32, kind="ExternalOutput").ap()
    W = {n: nc.dram_tensor(n, s, F32, kind="ExternalInput").ap() for n, s in WEIGHT_SPECS(L)}
    SO = smallp_layout(L)
    smallp_d = nc.dram_tensor("smallp", [128, SO["n"]], F32, kind="ExternalInput").ap()
    sgurep_d = nc.dram_tensor("sgurep", [L, 128, 3072], F32, kind="ExternalInput").ap()
    sguwT_d = nc.dram_tensor("sguwT", [L, 128, 1024], F32, kind="ExternalInput").ap()

    S = Sched(nc)

    with ExitStack() as st:
        def sb(name, shape, dt):
            return st.enter_context(nc.sbuf_tensor(name, shape, dt))

        resid = sb("resid", [128, 16, TM], F32)
        nT = sb("nT", [128, 16, TM], BF16)
        yT = sb("yT", [128, 16, TM], BF16)
        scrA = sb("scrA", [128, 16 * TM // 2], F32)
        scrA_bf = scrA.bitcast(BF16)
        ring = [sb(f"ring{i}", [128, SLABE], BF16) for i in range(NS)]
        smallp = sb("smallp_s", [128, SO["n"]], F32)
        sgurep = sb("sgurep_s", [128, 3072], F32)
        sguw = sb("sguw", [128, 1024], BF16)
        poolw = sb("poolw", [128, 512], BF16)
        hist = sb("hist", [128, L * 8 * 16], F32)
        pTs = sb("pTs", [128, 2, TM], BF16)
        rstd = sb("rstd", [128, TM], F32)
        sqt = [sb(f"sqt{i}", [128, 384], BF16) for i in range(2)]
        atmp = [sb(f"atmp{i}", [128, 512], F32) for i in range(4)]
        onesD = sb("onesD", [128, 128], BF16)
        mask = sb("mask", [128, 128], F32)
        lnst = sb("lnst", [128, 16], F32)
        epsc = sb("epsc", [128, 8], F32)
        psum = st.enter_context(nc.psum_tensor("psum", [128, 8 * 512], F32))

        r_res = [Res(f"res{c}") for c in range(16)]
        r_n = [Res(f"n{c}") for c in range(16)]
        r_y = [Res(f"y{c}") for c in range(16)]
        r_s = [Res(f"s{c}") for c in range(16)]
        r_ring = [Res(f"ring{i}") for i in range(NS)]
        r_small = Res("smallp")
        r_sgurep = Res("sgurep")
        r_sguw = Res("sguw")
        r_poolw = Res("poolw")
        r_hist = Res("hist")
        r_pTs = Res("pTs")
        r_rstd = Res("rstd")
        r_rh = [r_rstd, Res("rstd_h1")]
        r_sqt = [Res("sqt0"), Res("sqt1")]
        r_atmp = [Res(f"atmp{i}") for i in range(4)]
        r_const = Res("const")
        r_lnst = Res("lnst")
        r_bank = [Res(f"bank{i}") for i in range(8)]

        CH = TM // 2

        def sreg(lo, n):
            c0 = lo // CH
            c1 = (lo + n - 1) // CH
            return scrA[:, lo:lo + n], [r_s[c] for c in range(c0, c1 + 1)]

        HW_ = 16 + TM
        vtok0, rs_vtok0 = sreg(0, 1024)
        vtok, rs_vtok = vtok0, rs_vtok0
        _, rs_vln0 = sreg(1024, 512)
        _, rs_vln1 = sreg(1536, 512)
        vlnb = [(scrA_bf[:, 2048:3072], rs_vln0), (scrA_bf[:, 3072:4096], rs_vln1)]
        cx, rs_cx = sreg(2048, HW_)
        pin, rs_pin = sreg(2048 + HW_, HW_)
        pa, rs_pa = sreg(2048 + 2 * HW_, HW_)
        pb, rs_pb = sreg(2048 + 3 * HW_, HW_)
        o_pl = 2048 + 4 * HW_
        assert o_pl + TM // 2 <= 16 * CH, (o_pl, TM)
        pooledT = scrA_bf[:, 2 * o_pl:2 * o_pl + TM]
        _, rs_pooled = sreg(o_pl, TM // 2)

        def mTc(j, lo, n):
            return scrA_bf[:, j * TM + lo:j * TM + lo + n]

        bank_ctr = [0]

        def bank(n=1):
            b = bank_ctr[0]
            if n == 2 and b % 2 == 1:
                b += 1
            b %= 6
            bank_ctr[0] = b + n
            return b, [r_bank[b + i] for i in range(n)]

        def PS(b, lo, n):
            return psum[:, b * 512 + lo:b * 512 + lo + n]

        def mm(out, lhsT, rhs, start, stop, reads, writes):
            S.op("pe", lambda e: e.matmul(out, lhsT=lhsT, rhs=rhs, start=start, stop=stop), reads, writes)

        def act(out, in_, func, reads, writes, scale=None):
            if scale is None:
                S.op("act", lambda e: e.activation(out=out, in_=in_, func=func), reads, writes)
            else:
                S.op("act", lambda e: e.activation(out=out, in_=in_, func=func, scale=scale), reads, writes)

        def tt(out, in0, in1, op, reads, writes, eng="dve"):
            S.op(eng, lambda e: e.tensor_tensor(out=out, in0=in0, in1=in1, op=op), reads, writes)

        def stt(out, in0, scalar, in1, op0, op1, reads, writes, eng="dve"):
            S.op(eng, lambda e: e.scalar_tensor_tensor(out=out, in0=in0, scalar=scalar, in1=in1, op0=op0, op1=op1),
                 reads, writes)

        def ts(out, in0, s1, s2, op0, op1, reads, writes, eng="dve"):
            if s2 is None:
                S.op(eng, lambda e: e.tensor_scalar(out=out, in0=in0, scalar1=s1, scalar2=None, op0=op0), reads, writes)
            else:
                S.op(eng, lambda e: e.tensor_scalar(out=out, in0=in0, scalar1=s1, scalar2=s2, op0=op0, op1=op1),
                     reads, writes)

        def rsqrt(out, in_, reads, writes):
            S.op("act", lambda e: e.activation(out=out, in_=in_, func=AF.Sqrt, bias=epsc[:, 0:1]), list(reads) + [r_const], writes)
            S.op("dve", lambda e: e.reciprocal(out=out, in_=out), writes, writes)

        def cp(out, in_, reads, writes, eng="dve"):
            S.op(eng, lambda e: e.tensor_copy(out=out, in_=in_), reads, writes)

        slabs = []

        def slab_cols(w, l, c0, width=SLABW, k=16):
            return [(0, k, width, w[l, :, c0:c0 + width])]

        def slab_rows(w, l, r0, k, width):
            return [(0, k, width, w[l, r0:r0 + k * 128, :])]

        def gen_slabs():
            for _st in range(len(TTS)):
                for l in range(L):
                    for pre in PHASES:
                        if pre in ("ffn1", "ffn2"):
                            wg, wu, wd = W[pre + "_w_gate"], W[pre + "_w_up"], W[pre + "_w_down"]
                            for fg in range(FF // 512):
                                c0 = fg * 512
                                slabs.append(slab_cols(wg, l, c0))
                                slabs.append(slab_cols(wu, l, c0))
                                slabs.append(slab_cols(wg, l, c0 + 256))
                                slabs.append(slab_cols(wu, l, c0 + 256))
                                slabs.append(slab_rows(wd, l, c0, 2, 2048))
                                slabs.append(slab_rows(wd, l, c0 + 256, 2, 2048))
                        elif pre == "mix":
                            wi = W["w_in"]
                            for sbi in range(8):
                                slabs.append(slab_cols(wi, l, sbi * 256))
                            for pr in range(2):
                                slabs.append(slab_cols(wi, l, 2560 + pr * 256))
                                slabs.append(slab_cols(wi, l, 3072 + pr * 256))
                                slabs.append(slab_cols(wi, l, 2048 + pr * 256))
                            for pr in range(2):
                                slabs.append(slab_cols(wi, l, 3584 + pr * 256))
                            for jp in range(8):
                                for k in range(3):
                                    slabs.append(slab_cols(wi, l, 4096 + k * 2048 + jp * 256))
                                slabs.append([
                                    (0, 8, 256, W["w_branch_a"][l, :, jp * 256:(jp + 1) * 256]),
                                    (8 * 256, 4, 256, W["w_branch_b"][l, :, jp * 256:(jp + 1) * 256]),
                                    (12 * 256, 4, 256, W["w_branch_c"][l, :, jp * 256:(jp + 1) * 256]),
                                ])
                            for sbi in range(8):
                                slabs.append(slab_cols(W["w_out"], l, sbi * 256))
                        else:
                            for sbi in range(8):
                                slabs.append([(0, 2, 256, W["ple_w_proj"][l, :, sbi * 256:(sbi + 1) * 256])])
                                slabs.append(slab_cols(W["ple_w_gate"], l, sbi * 256))

        gen_slabs()
        ring_state = {"issued": 0, "next": 0}

        def issue_slab():
            i = ring_state["issued"]
            if i >= len(slabs):
                return
            slot = i % NS
            for (o, k, width, src) in slabs[i]:
                dst = ring[slot][:, o:o + k * width].rearrange("p (k f) -> p k f", k=k)
                S.dma("pool", f"ring{slot}", dst, src.rearrange("(k p) f -> p k f", p=128), writes=[r_ring[slot]])
            ring_state["issued"] = i + 1

        def next_slab():
            i = ring_state["next"]
            ring_state["next"] = i + 1
            assert i < ring_state["issued"], "ring underflow (need more slots resident than NS)"
            slot = i % NS
            return ring[slot], r_ring[slot]

        def release(n=1):
            for _ in range(n):
                issue_slab()

        S.dma("sp", "small", smallp[:, :], smallp_d[:, :], writes=[r_small])
        S.op("dve", lambda e: e.memset(onesD[:, :], 1.0 / D), [], [r_const])
        S.op("dve", lambda e: e.memset(epsc[:, :], EPS), [], [r_const])
        S.op("dve", lambda e: e.memset(mask[:, :], 1.0), [], [r_const])
        S.op("dve", lambda e: e.memset(mask[64:128, 0:64], 0.0), [], [r_const])
        for _ in range(NS):
            issue_slab()

        def gain_ap(l, n, c):
            o = SO["gain"] + (l * 4 + n) * 16 + c
            return smallp[:, o:o + 1]

        def coltiles(TT):
            h = TT // 2
            return [(0, h), (h, h)]

        STATB = (6, 7)
        fz = {"pending": [], "cnt": [0, 0]}

        def stat_add(j, ti, c0, n):
            act(yT[:, j, c0:c0 + n], resid[:, j, c0:c0 + n], AF.Square, [r_res[j]], [r_y[j]])
            fz["pending"].append((j, ti, c0, n))

        def stat_flush(keep):
            while len(fz["pending"]) > keep:
                j, ti, c0, n = fz["pending"].pop(0)
                k = fz["cnt"][ti]
                mm(PS(STATB[ti], 0, n), onesD[:, :], yT[:, j, c0:c0 + n], k == 0, k == 15,
                   [r_const, r_y[j]], [r_bank[STATB[ti]]])
                fz["cnt"][ti] = (k + 1) % 16

        def rmsnorm(TT, gain_of_c, out_fp32_inplace=False):
            stat_flush(0)
            assert fz["cnt"] == [0, 0]
            for ti, (c0, n) in enumerate(coltiles(TT)):
                rsqrt(rstd[:, c0:c0 + n], PS(STATB[ti], 0, n), [r_bank[STATB[ti]]], r_rh)
                for c in range(16):
                    if out_fp32_inplace:
                        stt(resid[:, c, c0:c0 + n], resid[:, c, c0:c0 + n], gain_of_c(c), rstd[:, c0:c0 + n],
                            ALU.mult, ALU.mult, [r_res[c], r_small] + r_rh, [r_res[c]])
                    else:
                        stt(nT[:, c, c0:c0 + n], resid[:, c, c0:c0 + n], gain_of_c(c), rstd[:, c0:c0 + n],
                            ALU.mult, ALU.mult, [r_res[c], r_small] + r_rh, [r_n[c]])

        def ffn(TT, l, n_idx):
            rmsnorm(TT, lambda c: gain_ap(l, n_idx, c))
            ai = 0
            for fg in range(FF // 512):
                hb = fg % 2
                g01, rg01 = next_slab()
                u01, ru01 = next_slab()
                g23, rg23 = next_slab()
                u23, ru23 = next_slab()
                d01, rd01 = next_slab()
                d23, rd23 = next_slab()
                for q in range(4):
                    gs, rgs = (g01, rg01) if q < 2 else (g23, rg23)
                    us, rus = (u01, ru01) if q < 2 else (u23, ru23)
                    co = (q % 2) * 128
                    hc = hb * 4 + q
                    for (c0, n) in coltiles(TT):
                        bg, rbg = bank()
                        for kc in range(16):
                            mm(PS(bg, 0, n), gs[:, kc * 256 + co:kc * 256 + co + 128], nT[:, kc, c0:c0 + n],
                               kc == 0, kc == 15, [rgs, r_n[kc]], rbg)
                        bu, rbu = bank()
                        for kc in range(16):
                            mm(PS(bu, 0, n), us[:, kc * 256 + co:kc * 256 + co + 128], nT[:, kc, c0:c0 + n],
                               kc == 0, kc == 15, [rus, r_n[kc]], rbu)
                        a = ai % 4
                        ai += 1
                        act(atmp[a][:, :n], PS(bg, 0, n), AF.Silu, rbg, [r_atmp[a]])
                        tt(mTc(hc, c0, n), atmp[a][:, :n], PS(bu, 0, n), ALU.mult, [r_atmp[a]] + rbu, [r_s[hc]])
                    if q == 1 or q == 3:
                        release(2)
                for j in range(16):
                    for ti_, (c0, n) in enumerate(coltiles(TT)):
                        bd, rbd = bank()
                        for q in range(4):
                            ds, rds = (d01, rd01) if q < 2 else (d23, rd23)
                            o = (q % 2) * 2048 + j * 128
                            mm(PS(bd, 0, n), ds[:, o:o + 128], mTc(hb * 4 + q, c0, n), q == 0, q == 3,
                               [rds, r_s[hb * 4 + q]], rbd)
                        stt(resid[:, j, c0:c0 + n], PS(bd, 0, n), 0.5, resid[:, j, c0:c0 + n], ALU.mult, ALU.add,
                            rbd + [r_res[j]], [r_res[j]])
                        if fg == FF // 512 - 1:
                            stat_add(j, ti_, c0, n)
                            stat_flush(6)
                release(2)

        def hist_ap(l, i):
            o = (l * 8 + i) * 16
            return hist[:, o:o + 16]

        def mixer(TT, l, st_i):
            rmsnorm(TT, lambda c: gain_ap(l, 1, c))
            cts = coltiles(TT)
            nblk = TT // 128
            S.dma("sp", "sgurep", sgurep[:, :], sgurep_d[l, :, :], writes=[r_sgurep])
            S.dma("sp", "stage", vtok, sguwT_d[l, :, :], writes=rs_vtok)
            for g in range(8):
                tt(sguw[:, g * 128:(g + 1) * 128], vtok[:, g * 128:(g + 1) * 128], mask[:, :], ALU.mult,
                   rs_vtok + [r_const], [r_sguw])
            S.dma("sp", "stage2", pa[:, 0:512].rearrange("p (g d) -> p g d", g=4),
                  W["pool_w"][l].rearrange("g c d -> c g d"), writes=rs_pa)
            cp(poolw[:, :], pa[:, 0:512], rs_pa, [r_poolw])
            ai = 0
            for sbi in range(4):
                sl, rsl = next_slab()
                for cc in range(2):
                    c = sbi * 2 + cc
                    for (c0, n) in cts:
                        b, rb = bank()
                        for kc in range(16):
                            mm(PS(b, 0, n), sl[:, kc * 256 + cc * 128:kc * 256 + cc * 128 + 128], nT[:, kc, c0:c0 + n],
                               kc == 0, kc == 15, [rsl, r_n[kc]], rb)
                        act(yT[:, c, c0:c0 + n], PS(b, 0, n), AF.Gelu, rb, [r_y[c]])
                release(1)
            vs = [next_slab() for _ in range(4)]

            vtokb = [(vtok0, rs_vtok0), (scrA[:, 2048 + 2 * HW_:2048 + 2 * HW_ + 1024], rs_pa + rs_pb)]

            def m2_front(blk):
                t0 = blk * 128
                vln, rs_vln = vlnb[blk % 2]
                vtok, rs_vtok = vtokb[blk % 2]
                b, rb = bank(2)
                for sv in range(4):
                    sl, rsl = vs[sv]
                    for kc in range(16):
                        mm(PS(b, sv * 256, 256), nT[:, kc, t0:t0 + 128], sl[:, kc * 256:(kc + 1) * 256],
                           kc == 0, kc == 15, [rsl, r_n[kc]], rb)
                act(vtok, PS(b, 0, 1024), AF.Gelu, rb, rs_vtok)
                S.op("dve", lambda e: e.bn_stats(out=lnst[:, 0:6], in_=vtok[:, 0:512]), rs_vtok, [r_lnst])
                S.op("dve", lambda e: e.bn_stats(out=lnst[:, 6:12], in_=vtok[:, 512:1024]), rs_vtok, [r_lnst])
                S.op("dve", lambda e: e.bn_aggr(out=lnst[:, 12:14], in_=lnst[:, 0:12]), [r_lnst], [r_lnst])
                rsqrt(lnst[:, 14:15], lnst[:, 13:14], [r_lnst], [r_lnst])
                stt(lnst[:, 15:16], lnst[:, 12:13], -1.0, lnst[:, 14:15], ALU.mult, ALU.mult, [r_lnst], [r_lnst])
                S.op("act", lambda e: e.activation(out=vtok, in_=vtok, func=AF.Identity, bias=lnst[:, 15:16],
                                                   scale=lnst[:, 14:15]), rs_vtok + [r_lnst], rs_vtok)
                tt(vtok, vtok, sgurep[:, 0:1024], ALU.mult, rs_vtok + [r_sgurep], rs_vtok)
                tt(vln, vtok, sgurep[:, 1024:2048], ALU.add, rs_vtok + [r_sgurep], rs_vln)

            def m2_back(blk):
                nonlocal ai
                t0 = blk * 128
                vln, rs_vln = vlnb[blk % 2]
                b2, rb2 = bank(2)
                for g in range(8):
                    mm(PS(b2, g * 128, 128), vln[:, g * 128:(g + 1) * 128], sguw[:, g * 128:(g + 1) * 128], True, True,
                       rs_vln + [r_sguw], rb2)
                for h in range(2):
                    a = ai % 4
                    ai += 1
                    tt(atmp[a][:, :], PS(b2, h * 512, 512), sgurep[:, 2048 + h * 512:2048 + (h + 1) * 512], ALU.add,
                       rb2 + [r_sgurep], [r_atmp[a]])
                    yv = yT[:, 4 * h:4 * h + 4, t0:t0 + 128]
                    tt(yv, atmp[a][:, :].rearrange("p (g t) -> p g t", g=4), yv, ALU.mult,
                       [r_atmp[a]] + r_y[4 * h:4 * h + 4], r_y[4 * h:4 * h + 4])

            for blk in range(nblk + 1):
                if blk < nblk:
                    m2_front(blk)
                if blk >= 1:
                    m2_back(blk - 1)
            release(4)
            for pr in range(2):
                sC, rC = next_slab()
                sX, rX = next_slab()
                sB, rB = next_slab()
                cvb = [(pa, rs_pa), (pb, rs_pb)]
                for cc in range(2):
                    c = pr * 2 + cc
                    wo = cc * 128
                    cv, rs_cv = cvb[cc]
                    if st_i == 0:
                        S.op("dve", lambda e: e.memset(cx[:, 0:16], 0.0), [], rs_cx)
                    else:
                        cp(cx[:, 0:16], hist_ap(l, c), [r_hist], rs_cx)
                    for (c0, n) in cts:
                        bC, rbC = bank()
                        for kc in range(16):
                            mm(PS(bC, 0, n), sC[:, kc * 256 + wo:kc * 256 + wo + 128], nT[:, kc, c0:c0 + n],
                               kc == 0, kc == 15, [rC, r_n[kc]], rbC)
                        bX, rbX = bank()
                        for kc in range(16):
                            mm(PS(bX, 0, n), sX[:, kc * 256 + wo:kc * 256 + wo + 128], nT[:, kc, c0:c0 + n],
                               kc == 0, kc == 15, [rX, r_n[kc]], rbX)
                        a = ai % 4
                        ai += 1
                        act(atmp[a][:, :n], PS(bC, 0, n), AF.Copy, rbC, [r_atmp[a]])
                        tt(cx[:, 16 + c0:16 + c0 + n], atmp[a][:, :n], PS(bX, 0, n), ALU.mult, [r_atmp[a]] + rbX, rs_cx)
                    cp(hist_ap(l, c), cx[:, TT:TT + 16], rs_cx, [r_hist])
                    cwo = SO["conv"] + l * 12 + c * 3
                    ts(cv[:, 0:TT], cx[:, 16:16 + TT], smallp[:, cwo + 2:cwo + 3], None, ALU.mult, None,
                       rs_cx + [r_small], rs_cv)
                    stt(cv[:, 0:TT], cx[:, 15:15 + TT], smallp[:, cwo + 1:cwo + 2], cv[:, 0:TT], ALU.mult, ALU.add,
                        rs_cx + rs_cv + [r_small], rs_cv)
                    stt(cv[:, 0:TT], cx[:, 14:14 + TT], smallp[:, cwo:cwo + 1], cv[:, 0:TT], ALU.mult, ALU.add,
                        rs_cx + rs_cv + [r_small], rs_cv)
                for cc in range(2):
                    c = pr * 2 + cc
                    wo = cc * 128
                    cv, rs_cv = cvb[cc]
                    for (c0, n) in cts:
                        bB, rbB = bank()
                        for kc in range(16):
                            mm(PS(bB, 0, n), sB[:, kc * 256 + wo:kc * 256 + wo + 128], nT[:, kc, c0:c0 + n],
                               kc == 0, kc == 15, [rB, r_n[kc]], rbB)
                        tt(yT[:, 8 + c, c0:c0 + n], cv[:, c0:c0 + n], PS(bB, 0, n), ALU.mult, rs_cv + rbB, [r_y[8 + c]])
                release(3)
            def m4_back(g):
                pso = SO["pscale"] + l * 4 + g
                for (c0, n) in cts:
                    b, rb = bank()
                    mm(PS(b, 0, n), poolw[:, g * 128:(g + 1) * 128], pooledT[:, c0:c0 + n], True, True,
                       [r_poolw] + rs_pooled, rb)
                    act(yT[:, 12 + g, c0:c0 + n], PS(b, 0, n), AF.Copy, rb + [r_small], [r_y[12 + g]],
                        scale=smallp[:, pso:pso + 1])

            pending = None
            for pr in range(2):
                sP, rP = next_slab()
                for cc in range(2):
                    g = pr * 2 + cc
                    wo = cc * 128
                    wdw = WINDOWS[g]
                    if st_i == 0:
                        S.op("dve", lambda e: e.memset(pin[:, 0:16], 0.0), [], rs_pin)
                    else:
                        cp(pin[:, 0:16], hist_ap(l, 4 + g), [r_hist], rs_pin)
                    for (c0, n) in cts:
                        b, rb = bank()
                        for kc in range(16):
                            mm(PS(b, 0, n), sP[:, kc * 256 + wo:kc * 256 + wo + 128], nT[:, kc, c0:c0 + n],
                               kc == 0, kc == 15, [rP, r_n[kc]], rb)
                        act(pin[:, 16 + c0:16 + c0 + n], PS(b, 0, n), AF.Copy, rb, rs_pin)
                    cp(hist_ap(l, 4 + g), pin[:, TT:TT + 16], rs_pin, [r_hist])
                    if pending is not None:
                        m4_back(pending)
                    E_ = 16 + TT
                    tt(pa[:, 1:E_], pin[:, 1:E_], pin[:, 0:E_ - 1], ALU.add, rs_pin, rs_pa)
                    cur, rcur = pa, rs_pa
                    if wdw >= 4:
                        tt(pb[:, 3:E_], pa[:, 3:E_], pa[:, 1:E_ - 2], ALU.add, rs_pa, rs_pb)
                        cur, rcur = pb, rs_pb
                    if wdw >= 8:
                        tt(pa[:, 7:E_], pb[:, 7:E_], pb[:, 3:E_ - 4], ALU.add, rs_pb, rs_pa)
                        cur, rcur = pa, rs_pa
                    if wdw >= 16:
                        tt(pb[:, 15:E_], pa[:, 15:E_], pa[:, 7:E_ - 8], ALU.add, rs_pa, rs_pb)
                        cur, rcur = pb, rs_pb
                    stt(pooledT[:, 0:TT], cur[:, 16:E_], 1.0 / wdw, pin[:, 16:E_], ALU.mult, ALU.subtract,
                        rcur + rs_pin, rs_pooled)
                    if st_i == 0:
                        a = ai % 4
                        ai += 1
                        ro = SO["rc"] + g * 16
                        tt(atmp[a][:, 0:16], cur[:, 16:32], smallp[:, ro:ro + 16], ALU.mult, rcur + [r_small], [r_atmp[a]])
                        tt(pooledT[:, 0:16], atmp[a][:, 0:16], pin[:, 16:32], ALU.subtract, [r_atmp[a]] + rs_pin, rs_pooled)
                    pending = g
                release(1)
            m4_back(pending)
            KCS = (range(0, 8), range(8, 12), range(12, 16))
            hw = TM // 2
            tmpb = [(rstd[:, 0:hw], r_rh[0]), (rstd[:, hw:2 * hw], r_rh[1])]
            ti = 0
            for jp in range(8):
                G = [next_slab() for _ in range(3)]
                BR, rBR = next_slab()
                for k in range(3):
                    sl, rsl = G[k]
                    kcs = KCS[k]
                    for jj in range(2):
                        j = jp * 2 + jj
                        wo = jj * 128
                        for ti_, (c0, n) in enumerate(cts):
                            acc = jj * 2 + ti_
                            bg, rbg = bank()
                            for kc in range(16):
                                mm(PS(bg, 0, n), sl[:, kc * 256 + wo:kc * 256 + wo + 128], nT[:, kc, c0:c0 + n],
                                   kc == 0, kc == 15, [rsl, r_n[kc]], rbg)
                            bb, rbb = bank()
                            for kc in kcs:
                                mm(PS(bb, 0, n), BR[:, kc * 256 + wo:kc * 256 + wo + 128], yT[:, kc, c0:c0 + n],
                                   kc == kcs[0], kc == kcs[-1], [rBR, r_y[kc]], rbb)
                            tb, rtb = tmpb[ti % 2]
                            ti += 1
                            act(tb[:, :n], PS(bg, 0, n), AF.Sigmoid, rbg, [rtb])
                            if k == 0:
                                tt(atmp[acc][:, :n], tb[:, :n], PS(bb, 0, n), ALU.mult, [rtb] + rbb, [r_atmp[acc]])
                            else:
                                tt(tb[:, :n], tb[:, :n], PS(bb, 0, n), ALU.mult, [rtb] + rbb, [rtb])
                                if k == 1:
                                    tt(atmp[acc][:, :n], atmp[acc][:, :n], tb[:, :n], ALU.add,
                                       [rtb, r_atmp[acc]], [r_atmp[acc]])
                                else:
                                    tt(mTc(j, c0, n), atmp[acc][:, :n], tb[:, :n], ALU.add,
                                       [rtb, r_atmp[acc]], [r_s[j]])
                    release(1 if k < 2 else 2)
            for sbi in range(8):
                sl, rsl = next_slab()
                for jj in range(2):
                    j = sbi * 2 + jj
                    wo = jj * 128
                    for ti_, (c0, n) in enumerate(cts):
                        b, rb = bank()
                        for kc in range(16):
                            mm(PS(b, 0, n), sl[:, kc * 256 + wo:kc * 256 + wo + 128], mTc(kc, c0, n),
                               kc == 0, kc == 15, [rsl, r_s[kc]], rb)
                        tt(resid[:, j, c0:c0 + n], PS(b, 0, n), resid[:, j, c0:c0 + n], ALU.add, rb + [r_res[j]], [r_res[j]])
                        stat_add(j, ti_, c0, n)
                        stat_flush(3)
                release(1)

        def ple(TT, l, tok0):
            S.dma("pool", "pTs", pTs[:, :, 0:TT], pT[l, :, tok0:tok0 + TT].rearrange("(k p) t -> p k t", p=128),
                  writes=[r_pTs])
            rmsnorm(TT, lambda c: gain_ap(l, 3, c))
            ai = 0
            for sbi in range(8):
                pj, rpj = next_slab()
                sl, rsl = next_slab()
                for jj in range(2):
                    j = sbi * 2 + jj
                    wo = jj * 128
                    for ti_, (c0, n) in enumerate(coltiles(TT)):
                        bg, rbg = bank()
                        for kc in range(16):
                            mm(PS(bg, 0, n), sl[:, kc * 256 + wo:kc * 256 + wo + 128], nT[:, kc, c0:c0 + n],
                               kc == 0, kc == 15, [rsl, r_n[kc]], rbg)
                        bp, rbp = bank()
                        for kc in range(2):
                            mm(PS(bp, 0, n), pj[:, kc * 256 + wo:kc * 256 + wo + 128], pTs[:, kc, c0:c0 + n],
                               kc == 0, kc == 1, [rpj, r_pTs], rbp)
                        a = ai % 4
                        ai += 1
                        act(atmp[a][:, :n], PS(bg, 0, n), AF.Sigmoid, rbg, [r_atmp[a]])
                        tt(atmp[a][:, :n], atmp[a][:, :n], PS(bp, 0, n), ALU.mult, [r_atmp[a]] + rbp, [r_atmp[a]])
                        tt(resid[:, j, c0:c0 + n], atmp[a][:, :n], resid[:, j, c0:c0 + n], ALU.add,
                           [r_atmp[a], r_res[j]], [r_res[j]])
                        stat_add(j, ti_, c0, n)
                        stat_flush(3)
                release(2)

        tok0 = 0
        for st_i, TT in enumerate(TTS):
            S.dma("sp", "xin", resid[:, :, 0:TT], xT[:, tok0:tok0 + TT].rearrange("(c p) t -> p c t", p=128),
                  writes=r_res)
            for j in range(16):
                for ti_, (c0, n) in enumerate(coltiles(TT)):
                    stat_add(j, ti_, c0, n)
            for l in range(L):
                if "ffn1" in PHASES:
                    ffn(TT, l, 0)
                if "mix" in PHASES:
                    mixer(TT, l, st_i)
                if "ffn2" in PHASES:
                    ffn(TT, l, 2)
                if "ple" in PHASES:
                    ple(TT, l, tok0)
            fo = SO["final"]
            rmsnorm(TT, lambda c: smallp[:, fo + c:fo + c + 1], out_fp32_inplace=True)
            S.dma("sp", "xout", outT[:, tok0:tok0 + TT].rearrange("(c p) t -> p c t", p=128), resid[:, :, 0:TT],
                  reads=r_res)
            tok0 += TT
        assert ring_state["next"] == len(slabs), (ring_state, len(slabs))
        S.emit()
    return nc


def make_core_inputs(inputs, L, T, SEQ, n_batch):
    x = np.asarray(inputs["x"], dtype=np.float32)
    p = np.asarray(inputs["p"], dtype=np.float32)
    SO = smallp_layout(L)
    sm = np.zeros((128, SO["n"]), np.float32)
    names = ["ffn1_norm", "mix_norm", "ffn2_norm", "ple_norm"]
    for l in range(L):
        for n, nm in enumerate(names):
            g = np.asarray(inputs[nm], np.float32)[l].reshape(16, 128).T
            o = SO["gain"] + (l * 4 + n) * 16
            sm[:, o:o + 16] = g
        cw = np.asarray(inputs["conv_w"], np.float32)[l]
        o = SO["conv"] + l * 12
        sm[:, o:o + 12] = cw.reshape(3, 4, 128).transpose(2, 1, 0).reshape(128, 12)
        ps = np.asarray(inputs["pool_scale"], np.float32)[l]
        o = SO["pscale"] + l * 4
        sm[:, o:o + 4] = ps.reshape(4, 128).T
    sm[:, SO["final"]:SO["final"] + 16] = np.asarray(inputs["final_norm"], np.float32).reshape(16, 128).T
    rc_first = np.zeros((4, 16), np.float32)
    rc_other = np.zeros((4, 16), np.float32)
    for g, w in enumerate(WINDOWS):
        for i in range(16):
            rc_first[g, i] = 1.0 / min(i + 1, w)
            rc_other[g, i] = 1.0 / w
    sm_first = sm.copy()
    sm_first[:, SO["rc"]:SO["rc"] + 64] = rc_first.reshape(1, 64)
    sm_other = sm.copy()
    sm_other[:, SO["rc"]:SO["rc"] + 64] = rc_other.reshape(1, 64)

    g_ = np.asarray(inputs["sgu_norm_g"], np.float32)[:L]
    b_ = np.asarray(inputs["sgu_norm_b"], np.float32)[:L]
    sb_ = np.asarray(inputs["sgu_b"], np.float32)[:L].reshape(L, 1024)
    rep = np.concatenate([g_, b_, sb_], axis=1)
    sgurep = np.ascontiguousarray(np.broadcast_to(rep[:, None, :], (L, 128, 3072)))
    sw = np.asarray(inputs["sgu_w"], np.float32)[:L]
    sguwT = np.ascontiguousarray(sw.transpose(0, 3, 1, 2).reshape(L, 128, 1024))

    shared = {n: np.asarray(inputs[n], np.float32)[:L] for n, _ in WEIGHT_SPECS(L)}
    shared["sgurep"] = sgurep
    shared["sguwT"] = sguwT
    in_maps = []
    for b in range(n_batch):
        for h in range(2):
            t0 = 0 if h == 0 else SEQ - T
            m = dict(shared)
            m["xT"] = np.ascontiguousarray(x[b, t0:t0 + T, :].T)
            m["pT"] = np.ascontiguousarray(p[:L, b, t0:t0 + T, :].transpose(0, 2, 1))
            m["smallp"] = sm_first if h == 0 else sm_other
            in_maps.append(m)
    return in_maps


def run_config(inputs, L, TTS, SEQ, n_batch, trace=False, **bk):
    T = sum(TTS)
    nc = build_program(L, TTS, **bk)
    in_maps = make_core_inputs(inputs, L, T, SEQ, n_batch)
    ncores = len(in_maps)
    res = run_bass_kernel_spmd(nc, in_maps, core_ids=list(range(ncores)), **({"trace": True} if trace else {}))
    out = np.empty((n_batch, SEQ, D), np.float32)
    for b in range(n_batch):
        o0 = res.results[2 * b]["outT"]
        o1 = res.results[2 * b + 1]["outT"]
        out[b, 0:T, :] = o0.T
        out[b, T:SEQ, :] = o1[:, 2 * T - SEQ:].T
    return out, res


def kernel(**inputs):
    out, _ = run_config(inputs, 4, [768, 768, 640], 4096, 4)
    return out
```

```python
import numpy as np
from contextlib import ExitStack
import concourse.bass as bass
import concourse.mybir as mybir
from concourse.bass_utils import run_bass_kernel_spmd

F32 = mybir.dt.float32
BF16 = mybir.dt.bfloat16
AF = mybir.ActivationFunctionType
ALU = mybir.AluOpType

D = 2048
FF = 5632
DA = 1024
DB = 512
DC = 512
NIN = 10240
DPLE = 256
EPS = 1e-6
WINDOWS = (2, 4, 8, 16)
EPOCH = 60000
SLABW = 256
SLABE = 4096


class Res:
    __slots__ = ("name", "w", "r")

    def __init__(self, name):
        self.name = name
        self.w = None
        self.r = {}


class _Eng:
    def __init__(self, key):
        self.key = key
        self.count = 0
        self.waited = {}
        self.thunks = []


class Sched:
    ENGS = ("pe", "act", "dve", "pool", "sp")

    def __init__(self, nc):
        self.nc = nc
        self.E = {k: _Eng(k) for k in self.ENGS}
        self.dma_sems = {}

    def _deps(self, reads, writes):
        d = {}
        for R in reads:
            if R.w is not None:
                k, v = R.w
                if v > d.get(k, 0):
                    d[k] = v
        for R in writes:
            if R.w is not None:
                k, v = R.w
                if v > d.get(k, 0):
                    d[k] = v
            for k, v in R.r.items():
                if v > d.get(k, 0):
                    d[k] = v
        return d

    def _waits(self, E, deps, nosync_same):
        ws = []
        for k, v in deps.items():
            if E.waited.get(k, 0) >= v:
                continue
            if k == E.key and nosync_same:
                continue
            E.waited[k] = v
            ws.append((k, v))
        return ws

    def op(self, ename, fn, reads=(), writes=()):
        E = self.E[ename]
        ws = self._waits(E, self._deps(reads, writes), ename == "pe")
        E.count += 1
        c = E.count
        E.thunks.append((ws, fn, (ename, c)))
        for R in reads:
            R.r[ename] = c
        for R in writes:
            R.w = (ename, c)
            R.r = {}
        return c

    def dma(self, qname, dkey, out, in_, reads=(), writes=()):
        E = self.E[qname]
        deps = self._deps(reads, writes)
        deps.pop(("dma", dkey), None)
        ws = self._waits(E, deps, False)
        n = self.dma_sems.get(dkey, 0) + 1
        self.dma_sems[dkey] = n
        key = ("dma", dkey)
        E.thunks.append((ws, (lambda e, out=out, in_=in_: e.dma_start(out=out, in_=in_)), (key, n)))
        for R in reads:
            R.r[key] = n
        for R in writes:
            R.w = (key, n)
            R.r = {}
        return n

    def emit(self):
        nc = self.nc
        with ExitStack() as st:
            sems = {}
            for k in self.ENGS:
                E = self.E[k]
                ne = max(1, (E.count + EPOCH - 1) // EPOCH)
                for ep in range(ne):
                    sems[(k, ep)] = st.enter_context(nc.semaphore(f"s_{k}_{ep}"))
            for dk in self.dma_sems:
                sems[("dma", dk)] = st.enter_context(nc.semaphore(f"d_{dk}"))

            def sem_val(k, v):
                if isinstance(k, tuple):
                    return sems[k], 16 * v
                ep = (v - 1) // EPOCH
                return sems[(k, ep)], v - ep * EPOCH

            block = st.enter_context(nc.Block())

            def run(E):
                def body(eng):
                    for ws, fn, sig in E.thunks:
                        for k, v in ws:
                            s, val = sem_val(k, v)
                            eng.wait_ge(s, val)
                        ins = fn(eng)
                        k, c = sig
                        if isinstance(k, tuple):
                            ins.then_inc(sems[k], 16)
                        else:
                            s, _ = sem_val(k, c)
                            ins.then_inc(s, 1)
                    if E.key == "sp":
                        for dk, n in self.dma_sems.items():
                            eng.wait_ge(sems[("dma", dk)], 16 * n)
                return body

            block.tensor(run(self.E["pe"]))
            block.scalar(run(self.E["act"]))
            block.vector(run(self.E["dve"]))
            block.gpsimd(run(self.E["pool"]))
            block.sync(run(self.E["sp"]))


def smallp_layout(L):
    off = {}
    o = 0
    off["gain"] = o
    o += L * 4 * 16
    off["final"] = o
    o += 16
    off["conv"] = o
    o += L * 12
    off["pscale"] = o
    o += L * 4
    off["rc"] = o
    o += 64
    off["n"] = o
    return off


WEIGHT_SPECS = lambda L: [
    ("ffn1_w_gate", [L, D, FF]), ("ffn1_w_up", [L, D, FF]), ("ffn1_w_down", [L, FF, D]),
    ("w_in", [L, D, NIN]),
    ("w_branch_a", [L, DA, D]), ("w_branch_b", [L, DB, D]), ("w_branch_c", [L, DC, D]),
    ("w_out", [L, D, D]),
    ("ffn2_w_gate", [L, D, FF]), ("ffn2_w_up", [L, D, FF]), ("ffn2_w_down", [L, FF, D]),
    ("ple_w_gate", [L, D, D]), ("ple_w_proj", [L, DPLE, D]),
    ("pool_w", [L, 4, 128, 128]),
]


def build_program(L, TTS, NS=6, PHASES=("ffn1", "mix", "ffn2", "ple")):
    T = sum(TTS)
    TM = max(TTS)
    nc = bass.Bass("TRN2", target_bir_lowering=False)
    xT = nc.dram_tensor("xT", [D, T], F32, kind="ExternalInput").ap()
    pT = nc.dram_tensor("pT", [L, DPLE, T], F32, kind="ExternalInput").ap()
    outT = nc.dram_tensor("outT", [D, T], F32, kind="ExternalOutput").ap()
    W = {n: nc.dram_tensor(n, s, F32, kind="ExternalInput").ap() for n, s in WEIGHT_SPECS(L)}
    SO = smallp_layout(L)
    smallp_d = nc.dram_tensor("smallp", [128, SO["n"]], F32, kind="ExternalInput").ap()
    sgurep_d = nc.dram_tensor("sgurep", [L, 128, 3072], F32, kind="ExternalInput").ap()
    sguwT_d = nc.dram_tensor("sguwT", [L, 128, 1024], F32, kind="ExternalInput").ap()

    S = Sched(nc)

    with ExitStack() as st:
        def sb(name, shape, dt):
            return st.enter_context(nc.sbuf_tensor(name, shape, dt))

        resid = sb("resid", [128, 16, TM], F32)
        nT = sb("nT", [128, 16, TM], BF16)
        yT = sb("yT", [128, 16, TM], BF16)
        scrA = sb("scrA", [128, 16 * TM // 2], F32)
        scrA_bf = scrA.bitcast(BF16)
        ring = [sb(f"ring{i}", [128, SLABE], BF16) for i in range(NS)]
        smallp = sb("smallp_s", [128, SO["n"]], F32)
        sgurep = sb("sgurep_s", [128, 3072], F32)
        sguw = sb("sguw", [128, 1024], BF16)
        poolw = sb("poolw", [128, 512], BF16)
        hist = sb("hist", [128, L * 8 * 16], F32)
        pTs = sb("pTs", [128, 2, TM], BF16)
        rstd = sb("rstd", [128, TM], F32)
        sqt = [sb(f"sqt{i}", [128, 384], BF16) for i in range(2)]
        atmp = [sb(f"atmp{i}", [128, 512], F32) for i in range(4)]
        onesD = sb("onesD", [128, 128], BF16)
        mask = sb("mask", [128, 128], F32)
        lnst = sb("lnst", [128, 16], F32)
        epsc = sb("epsc", [128, 8], F32)
        psum = st.enter_context(nc.psum_tensor("psum", [128, 8 * 512], F32))

        r_res = [Res(f"res{c}") for c in range(16)]
        r_n = [[Res(f"n{c}_0"), Res(f"n{c}_1")] for c in range(16)]
        cur = {"half": TM // 2}

        def RN(kc, c0, n):
            h = cur["half"]
            return [r_n[kc][ti] for ti in range(c0 // h, (c0 + n - 1) // h + 1)]
        r_y = [Res(f"y{c}") for c in range(16)]
        r_s = [Res(f"s{c}") for c in range(16)]
        r_ring = [Res(f"ring{i}") for i in range(NS)]
        r_small = Res("smallp")
        r_sgurep = Res("sgurep")
        r_sguw = Res("sguw")
        r_poolw = Res("poolw")
        r_hist = Res("hist")
        r_pTs = Res("pTs")
        r_rstd = Res("rstd")
        r_rh = [r_rstd, Res("rstd_h1")]
        r_sqt = [Res("sqt0"), Res("sqt1")]
        r_atmp = [Res(f"atmp{i}") for i in range(4)]
        r_const = Res("const")
        r_lnst = Res("lnst")
        r_bank = [Res(f"bank{i}") for i in range(8)]

        CH = TM // 2

        def sreg(lo, n):
            c0 = lo // CH
            c1 = (lo + n - 1) // CH
            return scrA[:, lo:lo + n], [r_s[c] for c in range(c0, c1 + 1)]

        HW_ = 16 + TM
        vtok0, rs_vtok0 = sreg(0, 1024)
        vtok, rs_vtok = vtok0, rs_vtok0
        _, rs_vln0 = sreg(1024, 512)
        _, rs_vln1 = sreg(1536, 512)
        vlnb = [(scrA_bf[:, 2048:3072], rs_vln0), (scrA_bf[:, 3072:4096], rs_vln1)]
        cx, rs_cx = sreg(2048, HW_)
        pin, rs_pin = sreg(2048 + HW_, HW_)
        pa, rs_pa = sreg(2048 + 2 * HW_, HW_)
        pb, rs_pb = sreg(2048 + 3 * HW_, HW_)
        o_pl = 2048 + 4 * HW_
        assert o_pl + TM // 2 <= 16 * CH, (o_pl, TM)
        pooledT = scrA_bf[:, 2 * o_pl:2 * o_pl + TM]
        _, rs_pooled = sreg(o_pl, TM // 2)

        def mTc(j, lo, n):
            return scrA_bf[:, j * TM + lo:j * TM + lo + n]

        bank_ctr = [0]

        def bank(n=1):
            b = bank_ctr[0]
            if n == 2 and b % 2 == 1:
                b += 1
            b %= 6
            bank_ctr[0] = b + n
            return b, [r_bank[b + i] for i in range(n)]

        def PS(b, lo, n):
            return psum[:, b * 512 + lo:b * 512 + lo + n]

        def mm(out, lhsT, rhs, start, stop, reads, writes):
            S.op("pe", lambda e: e.matmul(out, lhsT=lhsT, rhs=rhs, start=start, stop=stop), reads, writes)

        def act(out, in_, func, reads, writes, scale=None):
            if scale is None:
                S.op("act", lambda e: e.activation(out=out, in_=in_, func=func), reads, writes)
            else:
                S.op("act", lambda e: e.activation(out=out, in_=in_, func=func, scale=scale), reads, writes)

        def tt(out, in0, in1, op, reads, writes, eng="dve"):
            S.op(eng, lambda e: e.tensor_tensor(out=out, in0=in0, in1=in1, op=op), reads, writes)

        def stt(out, in0, scalar, in1, op0, op1, reads, writes, eng="dve"):
            S.op(eng, lambda e: e.scalar_tensor_tensor(out=out, in0=in0, scalar=scalar, in1=in1, op0=op0, op1=op1),
                 reads, writes)

        def ts(out, in0, s1, s2, op0, op1, reads, writes, eng="dve"):
            if s2 is None:
                S.op(eng, lambda e: e.tensor_scalar(out=out, in0=in0, scalar1=s1, scalar2=None, op0=op0), reads, writes)
            else:
                S.op(eng, lambda e: e.tensor_scalar(out=out, in0=in0, scalar1=s1, scalar2=s2, op0=op0, op1=op1),
                     reads, writes)

        def rsqrt(out, in_, reads, writes):
            S.op("act", lambda e: e.activation(out=out, in_=in_, func=AF.Sqrt, bias=epsc[:, 0:1]), list(reads) + [r_const], writes)
            S.op("dve", lambda e: e.reciprocal(out=out, in_=out), writes, writes)

        def cp(out, in_, reads, writes, eng="dve"):
            S.op(eng, lambda e: e.tensor_copy(out=out, in_=in_), reads, writes)

        slabs = []

        def slab_cols(w, l, c0, width=SLABW, k=16):
            return [(0, k, width, w[l, :, c0:c0 + width])]

        def slab_rows(w, l, r0, k, width):
            return [(0, k, width, w[l, r0:r0 + k * 128, :])]

        def gen_slabs():
            for _st in range(len(TTS)):
                for l in range(L):
                    for pre in PHASES:
                        if pre in ("ffn1", "ffn2"):
                            wg, wu, wd = W[pre + "_w_gate"], W[pre + "_w_up"], W[pre + "_w_down"]
                            for fg in range(FF // 512):
                                c0 = fg * 512
                                slabs.append(slab_cols(wg, l, c0))
                                slabs.append(slab_cols(wu, l, c0))
                                slabs.append(slab_cols(wg, l, c0 + 256))
                                slabs.append(slab_cols(wu, l, c0 + 256))
                                slabs.append(slab_rows(wd, l, c0, 2, 2048))
                                slabs.append(slab_rows(wd, l, c0 + 256, 2, 2048))
                        elif pre == "mix":
                            wi = W["w_in"]
                            for sbi in range(8):
                                slabs.append(slab_cols(wi, l, sbi * 256))
                            for pr in range(2):
                                slabs.append(slab_cols(wi, l, 2560 + pr * 256))
                                slabs.append(slab_cols(wi, l, 3072 + pr * 256))
                                slabs.append(slab_cols(wi, l, 2048 + pr * 256))
                            for pr in range(2):
                                slabs.append(slab_cols(wi, l, 3584 + pr * 256))
                            for jp in range(8):
                                for k in range(3):
                                    slabs.append(slab_cols(wi, l, 4096 + k * 2048 + jp * 256))
                                slabs.append([
                                    (0, 8, 256, W["w_branch_a"][l, :, jp * 256:(jp + 1) * 256]),
                                    (8 * 256, 4, 256, W["w_branch_b"][l, :, jp * 256:(jp + 1) * 256]),
                                    (12 * 256, 4, 256, W["w_branch_c"][l, :, jp * 256:(jp + 1) * 256]),
                                ])
                            for sbi in range(8):
                                slabs.append(slab_cols(W["w_out"], l, sbi * 256))
                        else:
                            for sbi in range(8):
                                slabs.append([(0, 2, 256, W["ple_w_proj"][l, :, sbi * 256:(sbi + 1) * 256])])
                                slabs.append(slab_cols(W["ple_w_gate"], l, sbi * 256))

        gen_slabs()
        ring_state = {"issued": 0, "next": 0}

        def issue_slab():
            i = ring_state["issued"]
            if i >= len(slabs):
                return
            slot = i % NS
            for (o, k, width, src) in slabs[i]:
                dst = ring[slot][:, o:o + k * width].rearrange("p (k f) -> p k f", k=k)
                S.dma("pool", f"ring{slot}", dst, src.rearrange("(k p) f -> p k f", p=128), writes=[r_ring[slot]])
            ring_state["issued"] = i + 1

        def next_slab():
            i = ring_state["next"]
            ring_state["next"] = i + 1
            assert i < ring_state["issued"], "ring underflow (need more slots resident than NS)"
            slot = i % NS
            return ring[slot], r_ring[slot]

        def release(n=1):
            for _ in range(n):
                issue_slab()

        S.dma("sp", "small", smallp[:, :], smallp_d[:, :], writes=[r_small])
        S.op("dve", lambda e: e.memset(onesD[:, :], 1.0 / D), [], [r_const])
        S.op("dve", lambda e: e.memset(epsc[:, :], EPS), [], [r_const])
        S.op("dve", lambda e: e.memset(mask[:, :], 1.0), [], [r_const])
        S.op("dve", lambda e: e.memset(mask[64:128, 0:64], 0.0), [], [r_const])
        for _ in range(NS):
            issue_slab()

        def gain_ap(l, n, c):
            o = SO["gain"] + (l * 4 + n) * 16 + c
            return smallp[:, o:o + 1]

        def coltiles(TT):
            h = TT // 2
            return [(0, h), (h, h)]

        STATB = (6, 7)
        fz = {"pending": [], "cnt": [0, 0]}

        def stat_add(j, ti, c0, n):
            act(yT[:, j, c0:c0 + n], resid[:, j, c0:c0 + n], AF.Square, [r_res[j]], [r_y[j]])
            fz["pending"].append((j, ti, c0, n))

        def stat_flush(keep):
            while len(fz["pending"]) > keep:
                j, ti, c0, n = fz["pending"].pop(0)
                k = fz["cnt"][ti]
                mm(PS(STATB[ti], 0, n), onesD[:, :], yT[:, j, c0:c0 + n], k == 0, k == 15,
                   [r_const, r_y[j]], [r_bank[STATB[ti]]])
                fz["cnt"][ti] = (k + 1) % 16

        def rmsnorm(TT, gain_of_c, out_fp32_inplace=False):
            stat_flush(0)
            assert fz["cnt"] == [0, 0]
            for ti, (c0, n) in enumerate(coltiles(TT)):
                rsqrt(rstd[:, c0:c0 + n], PS(STATB[ti], 0, n), [r_bank[STATB[ti]]], r_rh)
                for c in range(16):
                    if out_fp32_inplace:
                        stt(resid[:, c, c0:c0 + n], resid[:, c, c0:c0 + n], gain_of_c(c), rstd[:, c0:c0 + n],
                            ALU.mult, ALU.mult, [r_res[c], r_small] + r_rh, [r_res[c]])
                    else:
                        stt(nT[:, c, c0:c0 + n], resid[:, c, c0:c0 + n], gain_of_c(c), rstd[:, c0:c0 + n],
                            ALU.mult, ALU.mult, [r_res[c], r_small] + r_rh, [r_n[c][ti]])

        def ffn(TT, l, n_idx):
            rmsnorm(TT, lambda c: gain_ap(l, n_idx, c))
            ai = 0
            for fg in range(FF // 512):
                hb = fg % 2
                g01, rg01 = next_slab()
                u01, ru01 = next_slab()
                g23, rg23 = next_slab()
                u23, ru23 = next_slab()
                d01, rd01 = next_slab()
                d23, rd23 = next_slab()
                for q in range(4):
                    gs, rgs = (g01, rg01) if q < 2 else (g23, rg23)
                    us, rus = (u01, ru01) if q < 2 else (u23, ru23)
                    co = (q % 2) * 128
                    hc = hb * 4 + q
                    for (c0, n) in coltiles(TT):
                        bg, rbg = bank()
                        for kc in range(16):
                            mm(PS(bg, 0, n), gs[:, kc * 256 + co:kc * 256 + co + 128], nT[:, kc, c0:c0 + n],
                               kc == 0, kc == 15, [rgs] + RN(kc, c0, n), rbg)
                        bu, rbu = bank()
                        for kc in range(16):
                            mm(PS(bu, 0, n), us[:, kc * 256 + co:kc * 256 + co + 128], nT[:, kc, c0:c0 + n],
                               kc == 0, kc == 15, [rus] + RN(kc, c0, n), rbu)
                        a = ai % 4
                        ai += 1
                        act(atmp[a][:, :n], PS(bg, 0, n), AF.Silu, rbg, [r_atmp[a]])
                        tt(mTc(hc, c0, n), atmp[a][:, :n], PS(bu, 0, n), ALU.mult, [r_atmp[a]] + rbu, [r_s[hc]])
                    if q == 1 or q == 3:
                        release(2)
                for j in range(16):
                    for ti_, (c0, n) in enumerate(coltiles(TT)):
                        bd, rbd = bank()
                        for q in range(4):
                            ds, rds = (d01, rd01) if q < 2 else (d23, rd23)
                            o = (q % 2) * 2048 + j * 128
                            mm(PS(bd, 0, n), ds[:, o:o + 128], mTc(hb * 4 + q, c0, n), q == 0, q == 3,
                               [rds, r_s[hb * 4 + q]], rbd)
                        stt(resid[:, j, c0:c0 + n], PS(bd, 0, n), 0.5, resid[:, j, c0:c0 + n], ALU.mult, ALU.add,
                            rbd + [r_res[j]], [r_res[j]])
                        if fg == FF // 512 - 1:
                            stat_add(j, ti_, c0, n)
                            stat_flush(6)
                release(2)

        def hist_ap(l, i):
            o = (l * 8 + i) * 16
            return hist[:, o:o + 16]

        def mixer(TT, l, st_i):
            rmsnorm(TT, lambda c: gain_ap(l, 1, c))
            cts = coltiles(TT)
            nblk = TT // 128
            S.dma("sp", "sgurep", sgurep[:, :], sgurep_d[l, :, :], writes=[r_sgurep])
            S.dma("sp", "stage", vtok, sguwT_d[l, :, :], writes=rs_vtok)
            for g in range(8):
                tt(sguw[:, g * 128:(g + 1) * 128], vtok[:, g * 128:(g + 1) * 128], mask[:, :], ALU.mult,
                   rs_vtok + [r_const], [r_sguw])
            S.dma("sp", "stage2", pa[:, 0:512].rearrange("p (g d) -> p g d", g=4),
                  W["pool_w"][l].rearrange("g c d -> c g d"), writes=rs_pa)
            cp(poolw[:, :], pa[:, 0:512], rs_pa, [r_poolw])
            ai = 0
            for sbi in range(4):
                sl, rsl = next_slab()
                for cc in range(2):
                    c = sbi * 2 + cc
                    for (c0, n) in cts:
                        b, rb = bank()
                        for kc in range(16):
                            mm(PS(b, 0, n), sl[:, kc * 256 + cc * 128:kc * 256 + cc * 128 + 128], nT[:, kc, c0:c0 + n],
                               kc == 0, kc == 15, [rsl] + RN(kc, c0, n), rb)
                        act(yT[:, c, c0:c0 + n], PS(b, 0, n), AF.Gelu, rb, [r_y[c]])
                release(1)
            vs = [next_slab() for _ in range(4)]

            vtokb = [(vtok0, rs_vtok0), (scrA[:, 2048 + 2 * HW_:2048 + 2 * HW_ + 1024], rs_pa + rs_pb)]

            def m2_front(blk):
                t0 = blk * 128
                vln, rs_vln = vlnb[blk % 2]
                vtok, rs_vtok = vtokb[blk % 2]
                b, rb = bank(2)
                for sv in range(4):
                    sl, rsl = vs[sv]
                    for kc in range(16):
                        mm(PS(b, sv * 256, 256), nT[:, kc, t0:t0 + 128], sl[:, kc * 256:(kc + 1) * 256],
                           kc == 0, kc == 15, [rsl] + RN(kc, t0, 128), rb)
                act(vtok, PS(b, 0, 1024), AF.Gelu, rb, rs_vtok)
                S.op("dve", lambda e: e.bn_stats(out=lnst[:, 0:6], in_=vtok[:, 0:512]), rs_vtok, [r_lnst])
                S.op("dve", lambda e: e.bn_stats(out=lnst[:, 6:12], in_=vtok[:, 512:1024]), rs_vtok, [r_lnst])
                S.op("dve", lambda e: e.bn_aggr(out=lnst[:, 12:14], in_=lnst[:, 0:12]), [r_lnst], [r_lnst])
                rsqrt(lnst[:, 14:15], lnst[:, 13:14], [r_lnst], [r_lnst])
                stt(lnst[:, 15:16], lnst[:, 12:13], -1.0, lnst[:, 14:15], ALU.mult, ALU.mult, [r_lnst], [r_lnst])
                S.op("act", lambda e: e.activation(out=vtok, in_=vtok, func=AF.Identity, bias=lnst[:, 15:16],
                                                   scale=lnst[:, 14:15]), rs_vtok + [r_lnst], rs_vtok)
                tt(vtok, vtok, sgurep[:, 0:1024], ALU.mult, rs_vtok + [r_sgurep], rs_vtok)
                tt(vln, vtok, sgurep[:, 1024:2048], ALU.add, rs_vtok + [r_sgurep], rs_vln)

            def m2_back(blk):
                nonlocal ai
                t0 = blk * 128
                vln, rs_vln = vlnb[blk % 2]
                b2, rb2 = bank(2)
                for g in range(8):
                    mm(PS(b2, g * 128, 128), vln[:, g * 128:(g + 1) * 128], sguw[:, g * 128:(g + 1) * 128], True, True,
                       rs_vln + [r_sguw], rb2)
                for h in range(2):
                    a = ai % 4
                    ai += 1
                    tt(atmp[a][:, :], PS(b2, h * 512, 512), sgurep[:, 2048 + h * 512:2048 + (h + 1) * 512], ALU.add,
                       rb2 + [r_sgurep], [r_atmp[a]])
                    yv = yT[:, 4 * h:4 * h + 4, t0:t0 + 128]
                    tt(yv, atmp[a][:, :].rearrange("p (g t) -> p g t", g=4), yv, ALU.mult,
                       [r_atmp[a]] + r_y[4 * h:4 * h + 4], r_y[4 * h:4 * h + 4])

            for blk in range(nblk + 1):
                if blk < nblk:
                    m2_front(blk)
                if blk >= 1:
                    m2_back(blk - 1)
            release(4)
            for pr in range(2):
                sC, rC = next_slab()
                sX, rX = next_slab()
                sB, rB = next_slab()
                cvb = [(pa, rs_pa), (pb, rs_pb)]
                for cc in range(2):
                    c = pr * 2 + cc
                    wo = cc * 128
                    cv, rs_cv = cvb[cc]
                    if st_i == 0:
                        S.op("dve", lambda e: e.memset(cx[:, 0:16], 0.0), [], rs_cx)
                    else:
                        cp(cx[:, 0:16], hist_ap(l, c), [r_hist], rs_cx)
                    for (c0, n) in cts:
                        bC, rbC = bank()
                        for kc in range(16):
                            mm(PS(bC, 0, n), sC[:, kc * 256 + wo:kc * 256 + wo + 128], nT[:, kc, c0:c0 + n],
                               kc == 0, kc == 15, [rC] + RN(kc, c0, n), rbC)
                        bX, rbX = bank()
                        for kc in range(16):
                            mm(PS(bX, 0, n), sX[:, kc * 256 + wo:kc * 256 + wo + 128], nT[:, kc, c0:c0 + n],
                               kc == 0, kc == 15, [rX] + RN(kc, c0, n), rbX)
                        a = ai % 4
                        ai += 1
                        act(atmp[a][:, :n], PS(bC, 0, n), AF.Copy, rbC, [r_atmp[a]])
                        tt(cx[:, 16 + c0:16 + c0 + n], atmp[a][:, :n], PS(bX, 0, n), ALU.mult, [r_atmp[a]] + rbX, rs_cx)
                    cp(hist_ap(l, c), cx[:, TT:TT + 16], rs_cx, [r_hist])
                    cwo = SO["conv"] + l * 12 + c * 3
                    ts(cv[:, 0:TT], cx[:, 16:16 + TT], smallp[:, cwo + 2:cwo + 3], None, ALU.mult, None,
                       rs_cx + [r_small], rs_cv)
                    stt(cv[:, 0:TT], cx[:, 15:15 + TT], smallp[:, cwo + 1:cwo + 2], cv[:, 0:TT], ALU.mult, ALU.add,
                        rs_cx + rs_cv + [r_small], rs_cv)
                    stt(cv[:, 0:TT], cx[:, 14:14 + TT], smallp[:, cwo:cwo + 1], cv[:, 0:TT], ALU.mult, ALU.add,
                        rs_cx + rs_cv + [r_small], rs_cv)
                for cc in range(2):
                    c = pr * 2 + cc
                    wo = cc * 128
                    cv, rs_cv = cvb[cc]
                    for (c0, n) in cts:
                        bB, rbB = bank()
                        for kc in range(16):
                            mm(PS(bB, 0, n), sB[:, kc * 256 + wo:kc * 256 + wo + 128], nT[:, kc, c0:c0 + n],
                               kc == 0, kc == 15, [rB] + RN(kc, c0, n), rbB)
                        tt(yT[:, 8 + c, c0:c0 + n], cv[:, c0:c0 + n], PS(bB, 0, n), ALU.mult, rs_cv + rbB, [r_y[8 + c]])
                release(3)
            def m4_back(g):
                pso = SO["pscale"] + l * 4 + g
                for (c0, n) in cts:
                    b, rb = bank()
                    mm(PS(b, 0, n), poolw[:, g * 128:(g + 1) * 128], pooledT[:, c0:c0 + n], True, True,
                       [r_poolw] + rs_pooled, rb)
                    act(yT[:, 12 + g, c0:c0 + n], PS(b, 0, n), AF.Copy, rb + [r_small], [r_y[12 + g]],
                        scale=smallp[:, pso:pso + 1])

            pending = None
            for pr in range(2):
                sP, rP = next_slab()
                for cc in range(2):
                    g = pr * 2 + cc
                    wo = cc * 128
                    wdw = WINDOWS[g]
                    if st_i == 0:
                        S.op("dve", lambda e: e.memset(pin[:, 0:16], 0.0), [], rs_pin)
                    else:
                        cp(pin[:, 0:16], hist_ap(l, 4 + g), [r_hist], rs_pin)
                    for (c0, n) in cts:
                        b, rb = bank()
                        for kc in range(16):
                            mm(PS(b, 0, n), sP[:, kc * 256 + wo:kc * 256 + wo + 128], nT[:, kc, c0:c0 + n],
                               kc == 0, kc == 15, [rP] + RN(kc, c0, n), rb)
                        act(pin[:, 16 + c0:16 + c0 + n], PS(b, 0, n), AF.Copy, rb, rs_pin)
                    cp(hist_ap(l, 4 + g), pin[:, TT:TT + 16], rs_pin, [r_hist])
                    if pending is not None:
                        m4_back(pending)
                    E_ = 16 + TT
                    tt(pa[:, 1:E_], pin[:, 1:E_], pin[:, 0:E_ - 1], ALU.add, rs_pin, rs_pa)
                    cur, rcur = pa, rs_pa
                    if wdw >= 4:
                        tt(pb[:, 3:E_], pa[:, 3:E_], pa[:, 1:E_ - 2], ALU.add, rs_pa, rs_pb)
                        cur, rcur = pb, rs_pb
                    if wdw >= 8:
                        tt(pa[:, 7:E_], pb[:, 7:E_], pb[:, 3:E_ - 4], ALU.add, rs_pb, rs_pa)
                        cur, rcur = pa, rs_pa
                    if wdw >= 16:
                        tt(pb[:, 15:E_], pa[:, 15:E_], pa[:, 7:E_ - 8], ALU.add, rs_pa, rs_pb)
                        cur, rcur = pb, rs_pb
                    stt(pooledT[:, 0:TT], cur[:, 16:E_], 1.0 / wdw, pin[:, 16:E_], ALU.mult, ALU.subtract,
                        rcur + rs_pin, rs_pooled)
                    if st_i == 0:
                        a = ai % 4
                        ai += 1
                        ro = SO["rc"] + g * 16
                        tt(atmp[a][:, 0:16], cur[:, 16:32], smallp[:, ro:ro + 16], ALU.mult, rcur + [r_small], [r_atmp[a]])
                        tt(pooledT[:, 0:16], atmp[a][:, 0:16], pin[:, 16:32], ALU.subtract, [r_atmp[a]] + rs_pin, rs_pooled)
                    pending = g
                release(1)
            m4_back(pending)
            KCS = (range(0, 8), range(8, 12), range(12, 16))
            hw = TM // 2
            tmpb = [(rstd[:, 0:hw], r_rh[0]), (rstd[:, hw:2 * hw], r_rh[1])]
            ti = 0
            for jp in range(8):
                G = [next_slab() for _ in range(3)]
                BR, rBR = next_slab()
                for k in range(3):
                    sl, rsl = G[k]
                    kcs = KCS[k]
                    for jj in range(2):
                        j = jp * 2 + jj
                        wo = jj * 128
                        for ti_, (c0, n) in enumerate(cts):
                            acc = jj * 2 + ti_
                            bg, rbg = bank()
                            for kc in range(16):
                                mm(PS(bg, 0, n), sl[:, kc * 256 + wo:kc * 256 + wo + 128], nT[:, kc, c0:c0 + n],
                                   kc == 0, kc == 15, [rsl] + RN(kc, c0, n), rbg)
                            bb, rbb = bank()
                            for kc in kcs:
                                mm(PS(bb, 0, n), BR[:, kc * 256 + wo:kc * 256 + wo + 128], yT[:, kc, c0:c0 + n],
                                   kc == kcs[0], kc == kcs[-1], [rBR, r_y[kc]], rbb)
                            tb, rtb = tmpb[ti % 2]
                            ti += 1
                            act(tb[:, :n], PS(bg, 0, n), AF.Sigmoid, rbg, [rtb])
                            if k == 0:
                                tt(atmp[acc][:, :n], tb[:, :n], PS(bb, 0, n), ALU.mult, [rtb] + rbb, [r_atmp[acc]])
                            else:
                                tt(tb[:, :n], tb[:, :n], PS(bb, 0, n), ALU.mult, [rtb] + rbb, [rtb])
                                if k == 1:
                                    tt(atmp[acc][:, :n], atmp[acc][:, :n], tb[:, :n], ALU.add,
                                       [rtb, r_atmp[acc]], [r_atmp[acc]])
                                else:
                                    tt(mTc(j, c0, n), atmp[acc][:, :n], tb[:, :n], ALU.add,
                                       [rtb, r_atmp[acc]], [r_s[j]])
                    release(1 if k < 2 else 2)
            for sbi in range(8):
                sl, rsl = next_slab()
                for jj in range(2):
                    j = sbi * 2 + jj
                    wo = jj * 128
                    for ti_, (c0, n) in enumerate(cts):
                        b, rb = bank()
                        for kc in range(16):
                            mm(PS(b, 0, n), sl[:, kc * 256 + wo:kc * 256 + wo + 128], mTc(kc, c0, n),
                               kc == 0, kc == 15, [rsl, r_s[kc]], rb)
                        tt(resid[:, j, c0:c0 + n], PS(b, 0, n), resid[:, j, c0:c0 + n], ALU.add, rb + [r_res[j]], [r_res[j]])
                        stat_add(j, ti_, c0, n)
                        stat_flush(3)
                release(1)

        def ple(TT, l, tok0):
            S.dma("pool", "pTs", pTs[:, :, 0:TT], pT[l, :, tok0:tok0 + TT].rearrange("(k p) t -> p k t", p=128),
                  writes=[r_pTs])
            rmsnorm(TT, lambda c: gain_ap(l, 3, c))
            ai = 0
            for sbi in range(8):
                pj, rpj = next_slab()
                sl, rsl = next_slab()
                for jj in range(2):
                    j = sbi * 2 + jj
                    wo = jj * 128
                    for ti_, (c0, n) in enumerate(coltiles(TT)):
                        bg, rbg = bank()
                        for kc in range(16):
                            mm(PS(bg, 0, n), sl[:, kc * 256 + wo:kc * 256 + wo + 128], nT[:, kc, c0:c0 + n],
                               kc == 0, kc == 15, [rsl] + RN(kc, c0, n), rbg)
                        bp, rbp = bank()
                        for kc in range(2):
                            mm(PS(bp, 0, n), pj[:, kc * 256 + wo:kc * 256 + wo + 128], pTs[:, kc, c0:c0 + n],
                               kc == 0, kc == 1, [rpj, r_pTs], rbp)
                        a = ai % 4
                        ai += 1
                        act(atmp[a][:, :n], PS(bg, 0, n), AF.Sigmoid, rbg, [r_atmp[a]])
                        tt(atmp[a][:, :n], atmp[a][:, :n], PS(bp, 0, n), ALU.mult, [r_atmp[a]] + rbp, [r_atmp[a]])
                        tt(resid[:, j, c0:c0 + n], atmp[a][:, :n], resid[:, j, c0:c0 + n], ALU.add,
                           [r_atmp[a], r_res[j]], [r_res[j]])
                        stat_add(j, ti_, c0, n)
                        stat_flush(3)
                release(2)

        tok0 = 0
        for st_i, TT in enumerate(TTS):
            cur["half"] = TT // 2
            S.dma("sp", "xin", resid[:, :, 0:TT], xT[:, tok0:tok0 + TT].rearrange("(c p) t -> p c t", p=128),
                  writes=r_res)
            for j in range(16):
                for ti_, (c0, n) in enumerate(coltiles(TT)):
                    stat_add(j, ti_, c0, n)
            for l in range(L):
                if "ffn1" in PHASES:
                    ffn(TT, l, 0)
                if "mix" in PHASES:
                    mixer(TT, l, st_i)
                if "ffn2" in PHASES:
                    ffn(TT, l, 2)
                if "ple" in PHASES:
                    ple(TT, l, tok0)
            fo = SO["final"]
            rmsnorm(TT, lambda c: smallp[:, fo + c:fo + c + 1], out_fp32_inplace=True)
            S.dma("sp", "xout", outT[:, tok0:tok0 + TT].rearrange("(c p) t -> p c t", p=128), resid[:, :, 0:TT],
                  reads=r_res)
            tok0 += TT
        assert ring_state["next"] == len(slabs), (ring_state, len(slabs))
        S.emit()
    return nc


def make_core_inputs(inputs, L, T, SEQ, n_batch):
    x = np.asarray(inputs["x"], dtype=np.float32)
    p = np.asarray(inputs["p"], dtype=np.float32)
    SO = smallp_layout(L)
    sm = np.zeros((128, SO["n"]), np.float32)
    names = ["ffn1_norm", "mix_norm", "ffn2_norm", "ple_norm"]
    for l in range(L):
        for n, nm in enumerate(names):
            g = np.asarray(inputs[nm], np.float32)[l].reshape(16, 128).T
            o = SO["gain"] + (l * 4 + n) * 16
            sm[:, o:o + 16] = g
        cw = np.asarray(inputs["conv_w"], np.float32)[l]
        o = SO["conv"] + l * 12
        sm[:, o:o + 12] = cw.reshape(3, 4, 128).transpose(2, 1, 0).reshape(128, 12)
        ps = np.asarray(inputs["pool_scale"], np.float32)[l]
        o = SO["pscale"] + l * 4
        sm[:, o:o + 4] = ps.reshape(4, 128).T
    sm[:, SO["final"]:SO["final"] + 16] = np.asarray(inputs["final_norm"], np.float32).reshape(16, 128).T
    rc_first = np.zeros((4, 16), np.float32)
    rc_other = np.zeros((4, 16), np.float32)
    for g, w in enumerate(WINDOWS):
        for i in range(16):
            rc_first[g, i] = 1.0 / min(i + 1, w)
            rc_other[g, i] = 1.0 / w
    sm_first = sm.copy()
    sm_first[:, SO["rc"]:SO["rc"] + 64] = rc_first.reshape(1, 64)
    sm_other = sm.copy()
    sm_other[:, SO["rc"]:SO["rc"] + 64] = rc_other.reshape(1, 64)

    g_ = np.asarray(inputs["sgu_norm_g"], np.float32)[:L]
    b_ = np.asarray(inputs["sgu_norm_b"], np.float32)[:L]
    sb_ = np.asarray(inputs["sgu_b"], np.float32)[:L].reshape(L, 1024)
    rep = np.concatenate([g_, b_, sb_], axis=1)
    sgurep = np.ascontiguousarray(np.broadcast_to(rep[:, None, :], (L, 128, 3072)))
    sw = np.asarray(inputs["sgu_w"], np.float32)[:L]
    sguwT = np.ascontiguousarray(sw.transpose(0, 3, 1, 2).reshape(L, 128, 1024))

    shared = {n: np.asarray(inputs[n], np.float32)[:L] for n, _ in WEIGHT_SPECS(L)}
    shared["sgurep"] = sgurep
    shared["sguwT"] = sguwT
    in_maps = []
    for b in range(n_batch):
        for h in range(2):
            t0 = 0 if h == 0 else SEQ - T
            m = dict(shared)
            m["xT"] = np.ascontiguousarray(x[b, t0:t0 + T, :].T)
            m["pT"] = np.ascontiguousarray(p[:L, b, t0:t0 + T, :].transpose(0, 2, 1))
            m["smallp"] = sm_first if h == 0 else sm_other
            in_maps.append(m)
    return in_maps


def run_config(inputs, L, TTS, SEQ, n_batch, trace=False, **bk):
    T = sum(TTS)
    nc = build_program(L, TTS, **bk)
    in_maps = make_core_inputs(inputs, L, T, SEQ, n_batch)
    ncores = len(in_maps)
    res = run_bass_kernel_spmd(nc, in_maps, core_ids=list(range(ncores)), **({"trace": True} if trace else {}))
    out = np.empty((n_batch, SEQ, D), np.float32)
    for b in range(n_batch):
        o0 = res.results[2 * b]["outT"]
        o1 = res.results[2 * b + 1]["outT"]
        out[b, 0:T, :] = o0.T
        out[b, T:SEQ, :] = o1[:, 2 * T - SEQ:].T
    return out, res


def kernel(**inputs):
    out, _ = run_config(inputs, 4, [768, 768, 640], 4096, 4)
    return out
```
